# Optimizing a Trainium2 kernel written in Bass

```python
import math
import jax, jax.numpy as jnp
from jax import lax
import numpy as np

D_MODEL = 1024
BATCH = 1
SEQ = 16384
DEPTH = 2

GRID_W = 64
CTX_LEN = 256
HEAD_DIM = 64
MIX_WIDTH = D_MODEL
NA_WIDTH = MIX_WIDTH // 4
NA_HEADS = NA_WIDTH // HEAD_DIM
NA_WIN_ROWS = 8
NA_WIN_COLS = 16
SSD_WIDTH = MIX_WIDTH // 2
SSD_HEADDIM = 64
SSD_HEADS = SSD_WIDTH // SSD_HEADDIM
SSD_GROUPS = 2
SSD_HEADS_PER_GROUP = SSD_HEADS // SSD_GROUPS
SSD_STATE = 128
SSD_CONV = 5
SSD_CHUNK = 128
SSD_CONV_CH = SSD_WIDTH + 2 * SSD_GROUPS * SSD_STATE
GQA_WIDTH = MIX_WIDTH - NA_WIDTH - SSD_WIDTH
GQA_Q_HEADS = GQA_WIDTH // HEAD_DIM
GQA_KV_HEADS = GQA_Q_HEADS // 2
GQA_REP = GQA_Q_HEADS // GQA_KV_HEADS
Q_BLOCK = 128
ROPE_THETA = 10000.0
ROPE_PAIRS = HEAD_DIM // 4
NA_IN = 3 * NA_WIDTH
SSD_IN = SSD_WIDTH + SSD_CONV_CH + 2 * SSD_HEADS
GQA_IN = GQA_WIDTH + 2 * GQA_KV_HEADS * HEAD_DIM
IN_WIDTH = NA_IN + SSD_IN + GQA_IN
FFN_HIDDEN = -(-(8 * D_MODEL) // (3 * 256)) * 256
EPS = 1e-6

kernel_name = 'hybrid_na_ssd_gqa_dit_block'


def rms_norm(x, w):
    xf = x.astype(jnp.float32)
    y = xf * lax.rsqrt(jnp.mean(xf * xf, axis=-1, keepdims=True) + EPS)
    return y.astype(x.dtype) * w


def axial_rope(x, rows, cols):
    freqs = ROPE_THETA ** (-jnp.arange(ROPE_PAIRS, dtype=jnp.float32) / ROPE_PAIRS)

    def rotate(xh, pos):
        ang = pos[:, None] * freqs[None, :]
        ang = jnp.concatenate([ang, ang], axis=-1)[None, :, None, :]
        cos = jnp.cos(ang).astype(xh.dtype)
        sin = jnp.sin(ang).astype(xh.dtype)
        x1, x2 = jnp.split(xh, 2, axis=-1)
        return xh * cos + jnp.concatenate([-x2, x1], axis=-1) * sin

    half = HEAD_DIM // 2
    return jnp.concatenate([rotate(x[..., :half], rows), rotate(x[..., half:], cols)], axis=-1)


def gqa_softmax_attention(q, k, v):
    s = jnp.einsum('bqgrd,bkgd->bgrqk', q, k) * (HEAD_DIM ** -0.5)
    p = jax.nn.softmax(s.astype(jnp.float32), axis=-1).astype(v.dtype)
    return jnp.einsum('bgrqk,bkgd->bqgrd', p, v)


def neighbourhood_attention(u, u_c, rpb, ctx_out):
    b, l, _ = u.shape
    lc = u_c.shape[1]
    rows = l // GRID_W
    wh = min(NA_WIN_ROWS, rows)
    q, k, v = [t.reshape(b, rows, GRID_W, NA_HEADS, HEAD_DIM) for t in jnp.split(u, 3, axis=-1)]
    qc, kc, vc = [t.reshape(b, lc, NA_HEADS, HEAD_DIM) for t in jnp.split(u_c, 3, axis=-1)]
    scale = HEAD_DIM ** -0.5
    r = jnp.arange(rows)
    col = jnp.arange(GRID_W)
    r_start = jnp.clip(r - wh // 2, 0, rows - wh)
    rows_idx = r_start[:, None] + jnp.arange(wh)[None, :]
    c_start = jnp.clip(col - NA_WIN_COLS // 2, 0, GRID_W - NA_WIN_COLS)
    in_win = (col[None, :] >= c_start[:, None]) & (col[None, :] < c_start[:, None] + NA_WIN_COLS)
    dr = rows_idx - r[:, None] + NA_WIN_ROWS - 1
    dc = jnp.clip(col[None, :] - col[:, None] + NA_WIN_COLS - 1, 0, 2 * NA_WIN_COLS - 2)
    bias = rpb[:, dr[:, None, :, None], dc[None, :, None, :]].transpose(1, 0, 2, 3, 4)
    kb = k[:, rows_idx]
    vb = v[:, rows_idx]
    s_win = jnp.einsum('brqhd,brswhd->brhqsw', q, kb) * scale + bias[None]
    s_win = jnp.where(in_win[:, None, :], s_win, -jnp.inf)
    s_ctx = jnp.einsum('brqhd,bkhd->brhqk', q, kc) * scale
    n_win = wh * GRID_W
    s = jnp.concatenate([s_win.reshape(b, rows, NA_HEADS, GRID_W, n_win), s_ctx], axis=-1)
    p = jax.nn.softmax(s.astype(jnp.float32), axis=-1).astype(v.dtype)
    p_win = p[..., :n_win].reshape(b, rows, NA_HEADS, GRID_W, wh, GRID_W)
    p_ctx = p[..., n_win:]
    y = jnp.einsum('brhqsw,brswhd->brqhd', p_win, vb) + jnp.einsum('brhqk,bkhd->brqhd', p_ctx, vc)
    y = y.reshape(b, l, NA_WIDTH)
    y_c = None
    if ctx_out:
        y_c = gqa_softmax_attention(qc[:, :, :, None, :], kc, vc).reshape(b, lc, NA_WIDTH)
    return y, y_c


def depthwise_conv(x, w, bias):
    y = lax.conv_general_dilated(x, w[:, None, :], window_strides=(1,),
                                 padding=[(SSD_CONV // 2, SSD_CONV // 2)],
                                 dimension_numbers=('NWC', 'WIO', 'NWC'),
                                 feature_group_count=x.shape[-1])
    return y + bias


def ssd_scan(xs, dt, a, bm, cm, h0):
    b, l, h, p = xs.shape
    n = bm.shape[-1]
    nc = l // SSD_CHUNK
    xr = (xs * dt[..., None]).reshape(b, nc, SSD_CHUNK, h, p)
    br = bm.reshape(b, nc, SSD_CHUNK, h, n)
    cr = cm.reshape(b, nc, SSD_CHUNK, h, n)
    a_cum = jnp.cumsum((dt * a).reshape(b, nc, SSD_CHUNK, h), axis=2)
    seg = a_cum[:, :, :, None, :] - a_cum[:, :, None, :, :]
    causal = jnp.tril(jnp.ones((SSD_CHUNK, SSD_CHUNK), dtype=bool))[None, None, :, :, None]
    decay = jnp.exp(jnp.where(causal, seg, -jnp.inf))
    g = jnp.einsum('bcthn,bcshn->bctsh', cr, br) * decay
    y_diag = jnp.einsum('bctsh,bcshp->bcthp', g, xr)
    to_end = jnp.exp(a_cum[:, :, -1:, :] - a_cum)
    chunk_states = jnp.einsum('bcsh,bcshn,bcshp->bchpn', to_end, br, xr)
    chunk_decay = jnp.exp(a_cum[:, :, -1, :])

    def step(h_prev, inp):
        dec, st = inp
        return dec[:, :, None, None] * h_prev + st, h_prev

    h_final, h_enter = lax.scan(step, h0, (chunk_decay.transpose(1, 0, 2),
                                           chunk_states.transpose(1, 0, 2, 3, 4)))
    h_enter = h_enter.transpose(1, 0, 2, 3, 4)
    y_off = jnp.einsum('bcthn,bcth,bchpn->bcthp', cr, jnp.exp(a_cum), h_enter)
    return (y_diag + y_off).reshape(b, l, h, p), h_final


def ssd_mixer(u, u_c, conv_w, conv_b, dt_bias, a_log, d_skip, norm_w, ctx_out):
    f32 = jnp.float32
    a = -jnp.exp(a_log.astype(f32))

    def prep(v):
        b, l, _ = v.shape
        z, xbc, dt_raw = jnp.split(v, [SSD_WIDTH, SSD_WIDTH + SSD_CONV_CH], axis=-1)
        xbc = jax.nn.silu(depthwise_conv(xbc, conv_w, conv_b))
        xs, bm, cm = jnp.split(xbc, [SSD_WIDTH, SSD_WIDTH + SSD_GROUPS * SSD_STATE], axis=-1)
        xs = xs.reshape(b, l, SSD_HEADS, SSD_HEADDIM).astype(f32)
        bm = jnp.repeat(bm.reshape(b, l, SSD_GROUPS, SSD_STATE), SSD_HEADS_PER_GROUP, axis=2).astype(f32)
        cm = jnp.repeat(cm.reshape(b, l, SSD_GROUPS, SSD_STATE), SSD_HEADS_PER_GROUP, axis=2).astype(f32)
        dt = jax.nn.softplus(dt_raw.reshape(b, l, 2, SSD_HEADS).astype(f32) + dt_bias.astype(f32))
        return z, xs, bm, cm, dt

    def flip(t):
        return jnp.flip(t, axis=1)

    z, xs, bm, cm, dt = prep(u)
    zc, xsc, bmc, cmc, dtc = prep(u_c)
    h0 = jnp.zeros((u.shape[0], SSD_HEADS, SSD_HEADDIM, SSD_STATE), f32)
    yc_f, hc_f = ssd_scan(xsc, dtc[:, :, 0], a[0], bmc, cmc, h0)
    yc_b, hc_b = ssd_scan(flip(xsc), flip(dtc[:, :, 1]), a[1], flip(bmc), flip(cmc), h0)
    y_f, _ = ssd_scan(xs, dt[:, :, 0], a[0], bm, cm, hc_f)
    y_b, _ = ssd_scan(flip(xs), flip(dt[:, :, 1]), a[1], flip(bm), flip(cm), hc_b)

    def finish(yf, yb_flipped, xsd, zz):
        y = yf + flip(yb_flipped) + d_skip.astype(f32)[:, None] * xsd
        y = y.reshape(zz.shape).astype(zz.dtype) * jax.nn.silu(zz)
        return rms_norm(y, norm_w)

    y = finish(y_f, y_b, xs, z)
    y_c = finish(yc_f, yc_b, xsc, zc) if ctx_out else None
    return y, y_c


def gqa_attention(u, u_c, q_norm_w, k_norm_w, ctx_out):
    b, l, _ = u.shape
    lc = u_c.shape[1]
    kv_w = GQA_KV_HEADS * HEAD_DIM

    def prep(v, n):
        q, k, vv = jnp.split(v, [GQA_WIDTH, GQA_WIDTH + kv_w], axis=-1)
        q = rms_norm(q.reshape(b, n, GQA_Q_HEADS, HEAD_DIM), q_norm_w)
        k = rms_norm(k.reshape(b, n, GQA_KV_HEADS, HEAD_DIM), k_norm_w)
        return q, k, vv.reshape(b, n, GQA_KV_HEADS, HEAD_DIM)

    q, k, v = prep(u, l)
    qc, kc, vc = prep(u_c, lc)
    t = jnp.arange(l)
    rows = (t // GRID_W).astype(jnp.float32)
    cols = (t % GRID_W).astype(jnp.float32)
    q = axial_rope(q, rows, cols)
    k = axial_rope(k, rows, cols)
    k_all = jnp.concatenate([k, kc], axis=1)
    v_all = jnp.concatenate([v, vc], axis=1)
    nb = l // Q_BLOCK
    qb = q.reshape(b, nb, Q_BLOCK, GQA_KV_HEADS, GQA_REP, HEAD_DIM).transpose(1, 0, 2, 3, 4, 5)
    o = lax.map(lambda qi: gqa_softmax_attention(qi, k_all, v_all), qb)
    y = o.transpose(1, 0, 2, 3, 4, 5).reshape(b, l, GQA_WIDTH)
    y_c = None
    if ctx_out:
        qcg = qc.reshape(b, lc, GQA_KV_HEADS, GQA_REP, HEAD_DIM)
        y_c = gqa_softmax_attention(qcg, kc, vc).reshape(b, lc, GQA_WIDTH)
    return y, y_c


def swiglu(h, w_gate, w_up, w_down):
    return (jax.nn.silu(h @ w_gate) * (h @ w_up)) @ w_down


def setup_inputs(seed: int = 0) -> dict:
    key = jax.random.key(seed)
    ks = jax.random.split(key, 26)
    f32 = jnp.float32
    L = DEPTH

    def nrm(k, shape, scale):
        return jax.random.normal(k, shape, f32) * scale

    dt0 = jnp.exp(jax.random.uniform(ks[12], (L, 2, SSD_HEADS), f32, math.log(1e-3), math.log(1e-1)))
    return {
        'x': nrm(ks[0], (BATCH, SEQ, D_MODEL), 1.0),
        'c': nrm(ks[1], (BATCH, D_MODEL), 1.0),
        'ctx': nrm(ks[2], (BATCH, CTX_LEN, D_MODEL), 1.0),
        'c_ctx': nrm(ks[3], (D_MODEL,), 1.0),
        'mod_w': nrm(ks[4], (L, D_MODEL, 6 * D_MODEL), 0.5 * D_MODEL ** -0.5),
        'mod_b': nrm(ks[5], (L, 6 * D_MODEL), 0.01),
        'norm_attn_w': 1.0 + nrm(ks[6], (L, D_MODEL), 0.05),
        'norm_ffn_w': 1.0 + nrm(ks[7], (L, D_MODEL), 0.05),
        'w_in': nrm(ks[8], (L, D_MODEL, IN_WIDTH), D_MODEL ** -0.5),
        'na_rpb': nrm(ks[9], (L, NA_HEADS, 2 * NA_WIN_ROWS - 1, 2 * NA_WIN_COLS - 1), 0.1),
        'ssd_conv_w': nrm(ks[10], (L, SSD_CONV, SSD_CONV_CH), SSD_CONV ** -0.5),
        'ssd_conv_b': nrm(ks[11], (L, SSD_CONV_CH), 0.01),
        'ssd_dt_bias': dt0 + jnp.log(-jnp.expm1(-dt0)),
        'ssd_a_log': jnp.log(jax.random.uniform(ks[13], (L, 2, SSD_HEADS), f32, 1.0, 16.0)),
        'ssd_d': 1.0 + nrm(ks[14], (L, SSD_HEADS), 0.05),
        'ssd_norm_w': 1.0 + nrm(ks[15], (L, SSD_WIDTH), 0.05),
        'q_norm_w': 1.0 + nrm(ks[16], (L, HEAD_DIM), 0.05),
        'k_norm_w': 1.0 + nrm(ks[17], (L, HEAD_DIM), 0.05),
        'w_out': nrm(ks[18], (L, MIX_WIDTH, D_MODEL), MIX_WIDTH ** -0.5),
        'ffn_w_gate': nrm(ks[19], (L, D_MODEL, FFN_HIDDEN), D_MODEL ** -0.5),
        'ffn_w_up': nrm(ks[20], (L, D_MODEL, FFN_HIDDEN), D_MODEL ** -0.5),
        'ffn_w_down': nrm(ks[21], (L, FFN_HIDDEN, D_MODEL), FFN_HIDDEN ** -0.5),
        'final_norm_w': 1.0 + nrm(ks[22], (D_MODEL,), 0.05),
    }


def reference(x, c, ctx, c_ctx, mod_w, mod_b, norm_attn_w, norm_ffn_w, w_in, na_rpb,
              ssd_conv_w, ssd_conv_b, ssd_dt_bias, ssd_a_log, ssd_d, ssd_norm_w,
              q_norm_w, k_norm_w, w_out, ffn_w_gate, ffn_w_up, ffn_w_down, final_norm_w):
    cx = ctx
    splits = [NA_IN, NA_IN + SSD_IN]
    for i in range(DEPTH):
        ctx_out = i < DEPTH - 1
        mod = (jax.nn.silu(c) @ mod_w[i] + mod_b[i])[:, None, :]
        mod_c = (jax.nn.silu(c_ctx) @ mod_w[i] + mod_b[i])[None, None, :]
        sh_m, sc_m, g_m, sh_f, sc_f, g_f = jnp.split(mod, 6, axis=-1)
        csh_m, csc_m, cg_m, csh_f, csc_f, cg_f = jnp.split(mod_c, 6, axis=-1)
        h = rms_norm(x, norm_attn_w[i]) * (1.0 + sc_m) + sh_m
        hc = rms_norm(cx, norm_attn_w[i]) * (1.0 + csc_m) + csh_m
        ua, ub, ug = jnp.split(h @ w_in[i], splits, axis=-1)
        uca, ucb, ucg = jnp.split(hc @ w_in[i], splits, axis=-1)
        ya, yca = neighbourhood_attention(ua, uca, na_rpb[i], ctx_out)
        yb, ycb = ssd_mixer(ub, ucb, ssd_conv_w[i], ssd_conv_b[i], ssd_dt_bias[i], ssd_a_log[i],
                            ssd_d[i], ssd_norm_w[i], ctx_out)
        yg, ycg = gqa_attention(ug, ucg, q_norm_w[i], k_norm_w[i], ctx_out)
        x = x + g_m * (jnp.concatenate([ya, yb, yg], axis=-1) @ w_out[i])
        hf = rms_norm(x, norm_ffn_w[i]) * (1.0 + sc_f) + sh_f
        x = x + g_f * swiglu(hf, ffn_w_gate[i], ffn_w_up[i], ffn_w_down[i])
        if ctx_out:
            cx = cx + cg_m * (jnp.concatenate([yca, ycb, ycg], axis=-1) @ w_out[i])
            hcf = rms_norm(cx, norm_ffn_w[i]) * (1.0 + csc_f) + csh_f
            cx = cx + cg_f * swiglu(hcf, ffn_w_gate[i], ffn_w_up[i], ffn_w_down[i])
    return rms_norm(x, final_norm_w)
```

```python
import contextlib
import numpy as np
import ml_dtypes
import concourse.bass as bass
import concourse.mybir as mybir
from concourse.bass_utils import run_bass_kernel_spmd

F32 = mybir.dt.float32
BF16 = mybir.dt.bfloat16
AF = mybir.ActivationFunctionType
ALU = mybir.AluOpType
AX = mybir.AxisListType
NPBF = ml_dtypes.bfloat16

EPOCH = 20000
N_DMA_SEMS = 12
NCORE = 8
EPS = 1e-6


def _k(x):
    if isinstance(x, (str, tuple, int)):
        return x
    return x.name


class Sched:
    COMPUTE = ("pe", "act", "dve", "pool")

    def __init__(self, nc, stack):
        self.nc = nc
        self.stack = stack
        self.ops = []
        self.eng = {"pe": nc.tensor, "act": nc.scalar, "dve": nc.vector,
                    "pool": nc.gpsimd, "sp": nc.sync}
        self.sems = {}
        self.mcount = {q: 0 for q in self.COMPUTE}
        self.dcount = {}
        self.dma_rr = {q: 0 for q in self.eng}
        self.total = {q: 0 for q in self.eng}

    def _sem(self, key):
        if key not in self.sems:
            self.sems[key] = self.stack.enter_context(
                self.nc.semaphore("s_" + "_".join(str(x) for x in key)))
        return self.sems[key]

    def op(self, q, fn, reads=(), writes=(), dma=False):
        r = [_k(x) for x in reads]
        w = [_k(x) for x in writes]
        ex = [x for x in r if isinstance(x, str) and x.startswith("ps")]
        r = [x for x in r if x not in ex]
        w = w + [x for x in ex if x not in w]
        self.ops.append(dict(q=q, fn=fn, r=tuple(r), w=tuple(w), dma=dma))

    def pe(self, fn, reads=(), writes=()):
        self.op("pe", fn, reads, writes)

    def act(self, fn, reads=(), writes=()):
        self.op("act", fn, reads, writes)

    def dve(self, fn, reads=(), writes=()):
        self.op("dve", fn, reads, writes)

    def pool(self, fn, reads=(), writes=()):
        self.op("pool", fn, reads, writes)

    def dma(self, q, out, in_, reads=(), writes=(), **kw):
        self.op(q, lambda e: e.dma_start(out=out, in_=in_, **kw), reads, writes, dma=True)

    def flush(self):
        ops = self.ops
        self.ops = []
        n = len(ops)
        if n == 0:
            return
        last_w, readers = {}, {}
        deps = [None] * n
        for i, o in enumerate(ops):
            d = set()
            for r in o["r"]:
                if r in last_w:
                    d.add(last_w[r])
            for w in o["w"]:
                if w in last_w:
                    d.add(last_w[w])
                for j in readers.get(w, ()):
                    d.add(j)
            d.discard(i)
            deps[i] = d
            for r in o["r"]:
                readers.setdefault(r, []).append(i)
            for w in o["w"]:
                last_w[w] = i
                readers[w] = []
        pos = [0] * n
        cnt = {q: 0 for q in self.eng}
        for i, o in enumerate(ops):
            cnt[o["q"]] += 1
            pos[i] = cnt[o["q"]]
        known = {q: {s: 0 for s in self.COMPUTE} for q in self.eng}
        known_dma = {q: set() for q in self.eng}
        need = [None] * n
        marked = [False] * n
        dma_slot = {}
        prev_on_slot = {}
        for i, o in enumerate(ops):
            q = o["q"]
            w = []
            dl = set(deps[i])
            if o["dma"]:
                slot = (q, self.dma_rr[q] % N_DMA_SEMS)
                self.dma_rr[q] += 1
                dma_slot[i] = slot
                if slot in prev_on_slot:
                    dl.add(prev_on_slot[slot])
                prev_on_slot[slot] = i
            best = {}
            for j in dl:
                oj = ops[j]
                if oj["dma"]:
                    if j not in known_dma[q]:
                        w.append(j)
                        known_dma[q].add(j)
                else:
                    s = oj["q"]
                    if s == "pe" and q == "pe":
                        continue
                    if pos[j] > known[q][s]:
                        if s not in best or pos[j] > pos[best[s]]:
                            best[s] = j
            for s, j in best.items():
                w.append(j)
                known[q][s] = pos[j]
                marked[j] = True
            need[i] = w
        tail = {}
        for i, o in enumerate(ops):
            if not o["dma"]:
                tail[o["q"]] = i
        final = []
        for q, i in tail.items():
            if q in self.COMPUTE:
                marked[i] = True
                final.append(i)
        final += [i for i, o in enumerate(ops) if o["dma"]]
        val = [None] * n
        for i, o in enumerate(ops):
            if o["dma"]:
                slot = dma_slot[i]
                self.dcount[slot] = self.dcount.get(slot, 0) + 16
                val[i] = (("d",) + slot, self.dcount[slot])
            elif marked[i]:
                q = o["q"]
                c = self.mcount[q]
                self.mcount[q] += 1
                val[i] = (("c", q, c // EPOCH), c % EPOCH + 1)
        for i, o in enumerate(ops):
            e = self.eng[o["q"]]
            for j in need[i]:
                sk, v = val[j]
                e.wait_ge(self._sem(sk), v)
            ins = o["fn"](e)
            if val[i] is not None:
                sk, v = val[i]
                ins.then_inc(self._sem(sk), 16 if o["dma"] else 1)
        latest = {}
        for i in final:
            sk, v = val[i]
            latest[sk] = max(latest.get(sk, 0), v)
        for q, e in self.eng.items():
            for sk, v in latest.items():
                e.wait_ge(self._sem(sk), v)
        for q in cnt:
            self.total[q] += cnt[q]


import os
STOP = int(os.environ.get("A_STOP", "0"))


class _Stop(Exception):
    pass


def stop_check(s, n):
    if STOP == n:
        s.flush()
        raise _Stop()


class Rot:
    def __init__(self, items):
        self.items = list(items)
        self.i = 0

    def next(self):
        t = self.items[self.i % len(self.items)]
        self.i += 1
        return t


class Ctx:
    def __init__(self):
        self.nc = bass.Bass("TRN2", target_bir_lowering=False)
        self.stack = contextlib.ExitStack()
        self.s = Sched(self.nc, self.stack)
        self.ev = 0

    def din(self, name, shape, dt=F32):
        return self.nc.dram_tensor(name, list(shape), dt, kind="ExternalInput").ap()

    def dout(self, name, shape, dt=F32):
        return self.nc.dram_tensor(name, list(shape), dt, kind="ExternalOutput").ap()

    def sb(self, st, name, shape, dt=F32):
        return st.enter_context(self.nc.sbuf_tensor(name, list(shape), dt))

    def ps(self, st, name, shape, dt=F32):
        return st.enter_context(self.nc.psum_tensor(name, list(shape), dt))

    def evac_eng(self):
        self.ev += 1
        return "act" if self.ev % 2 else "dve"

    def copy(self, q, out, in_, reads, writes):
        if q == "act":
            self.s.act(lambda e: e.copy(out, in_), reads, writes)
        elif q == "dve":
            self.s.dve(lambda e: e.tensor_copy(out, in_), reads, writes)
        else:
            self.s.pool(lambda e: e.tensor_copy(out, in_), reads, writes)


D = 1024
SEQ = 16384
TOK = 2048
NT = 16
CTX = 256
NTC = 18
HB = 256
LAT = 2560
ALLT = LAT + CTX
INW = 2832
FFN = 2816
NCH = 22
C_NAQ, C_NAK, C_NAV = 0, 256, 512
C_Z, C_XBC, C_DT = 768, 1280, 2304
C_GQ, C_GV = 2320, 2704


def build_M():
    c = Ctx()
    nc, s = c.nc, c.s
    cs = c.din("cs", [128, 16])
    mw = c.din("mw", [2, 128, 8, 768])
    mb = c.din("mb", [2, 2, 768])
    out = c.dout("modo", [2, 2, 768])
    with contextlib.ExitStack() as st:
        cst = c.sb(st, "cst", [128, 16])
        sl = c.sb(st, "sl", [128, 16])
        wt = [c.sb(st, f"wt{i}", [128, 8, 768]) for i in range(2)]
        bt = c.sb(st, "bt", [2, 2, 768])
        ot = c.sb(st, "ot", [2, 2, 768])
        pm = [c.ps(st, f"psm{i}", [128, 512]) for i in range(4)]
        s.dma("sp", cst[:], cs, writes=[cst])
        s.dma("sp", bt[:], mb.rearrange("l r n -> r l n"), writes=[bt])
        for l in range(2):
            s.dma("sp" if l == 0 else "pool", wt[l][:], mw[l], writes=[wt[l]])
        s.act(lambda e: e.activation(sl[:], cst[:], AF.Silu), [cst], [sl])
        slv = sl[:].rearrange("p (c w) -> p c w", w=2)
        for l in range(2):
            for h in range(2):
                p = pm[l * 2 + h]
                for k in range(8):
                    s.pe(lambda e, p=p, l=l, h=h, k=k: e.matmul(
                        p[0:2, 0:384], slv[:, k, :], wt[l][:, k, h * 384:(h + 1) * 384],
                        start=(k == 0), stop=(k == 7)), [sl, wt[l]], [p])
                s.dve(lambda e, p=p, l=l, h=h: e.tensor_tensor(
                    ot[:, l, h * 384:(h + 1) * 384], p[0:2, 0:384],
                    bt[:, l, h * 384:(h + 1) * 384], ALU.add), [p, bt], [ot])
        s.dma("sp", out.rearrange("l r n -> r l n"), ot[:], reads=[ot], writes=["modo"])
        s.flush()
    c.stack.close()
    return nc


def run_M(inp):
    cvec = np.stack([inp["c"][0].reshape(8, 128).T, inp["c_ctx"].reshape(8, 128).T], axis=-1)
    cs = np.ascontiguousarray(cvec.reshape(128, 16)).astype(np.float32)
    maps = []
    for k in range(NCORE):
        sl = slice(768 * k, 768 * (k + 1))
        mw = inp["mod_w"][:, :, sl].reshape(2, 8, 128, 768).transpose(0, 2, 1, 3)
        mb = np.repeat(inp["mod_b"][:, None, sl], 2, axis=1)
        maps.append(dict(cs=cs, mw=np.ascontiguousarray(mw), mb=np.ascontiguousarray(mb)))
    res = run_bass_kernel_spmd(build_M(), maps, core_ids=list(range(NCORE)))
    mod = np.concatenate([r["modo"] for r in res.results], axis=-1)
    return mod


def tri_consts():
    r = np.arange(128)[:, None]
    t = np.arange(128)[None, :]
    L1 = (r > t).astype(np.float32)
    U = (r <= t).astype(np.float32)
    Lb = (r < t).astype(np.float32)
    Ub = (r >= t).astype(np.float32)
    ON = np.ones((128, 128), np.float32)
    ID = np.eye(128, dtype=np.float32)
    return np.ascontiguousarray(np.stack([L1, U, Lb, Ub, ON, ID], axis=1))


def rope_tables(core):
    t = np.arange(core * TOK, (core + 1) * TOK)
    rows = (t // 64).astype(np.float32)
    cols = (t % 64).astype(np.float32)
    freqs = (np.float32(10000.0) ** (-np.arange(16, dtype=np.float32) / np.float32(16))).astype(np.float32)

    def tab(pos):
        ang = (pos[:, None] * freqs[None, :]).astype(np.float32)
        ang = np.concatenate([ang, ang], axis=-1)
        co = np.cos(ang).astype(np.float32)
        si = np.sin(ang).astype(np.float32)
        si = np.concatenate([-si[:, :16], si[:, 16:]], axis=-1)
        return co, si
    cr, sr = tab(rows)
    cc, sc = tab(cols)
    cos = np.concatenate([cr, cc], axis=-1)
    sin = np.concatenate([sr, sc], axis=-1)
    tb = np.stack([cos, sin], axis=1)
    return np.ascontiguousarray(tb.reshape(NT, 128, 2, 64).transpose(1, 0, 2, 3))


def na_slot_row(core, s):
    l = s - 4
    if 0 <= l < 32:
        return 32 * core + l
    if l < 0:
        return 32 * core + l if core > 0 else s + 4
    if core < NCORE - 1:
        return 32 * core + l
    return 32 * core + 24 + (s - 36)


def na_bias_table(rpb, core):
    NEG = np.float32(-30000.0)
    q = np.arange(64)
    kc = np.arange(64)
    cstart = np.clip(q - 8, 0, 48)
    inwin = (kc[:, None] >= cstart[None, :]) & (kc[:, None] < cstart[None, :] + 16)
    dc = np.clip(kc[:, None] - q[None, :] + 15, 0, 30)
    tab = np.full((128, 8, 4, 4, 64), NEG, np.float32)
    pats = [(0, 8)] + [(1 + lr, lr) for lr in range(4)] + [(5 + i, 29 + i) for i in range(3)]
    for p, lr in pats:
        r = 32 * core + lr
        for j in range(4):
            for a in range(2):
                s = lr + 2 * j + a
                if p == 0:
                    dr = 2 * j + a + 3
                else:
                    dr = na_slot_row(core, s) - r + 7
                if not (0 <= dr <= 14):
                    continue
                for h in range(4):
                    b = rpb[h, dr][dc]
                    tab[a * 64:(a + 1) * 64, p, j, h, :] = np.where(inwin, b, NEG)
    return tab


def na_pattern(lr):
    if lr < 4:
        return 1 + lr
    if lr >= 29:
        return 5 + (lr - 29)
    return 0


def x_with_halo(xfull, core):
    x = xfull.reshape(256, 64, D)
    rows = [na_slot_row(core, s) for s in range(39)]
    out = np.zeros((40, 64, D), np.float32)
    out[:39] = x[rows]
    return out.reshape(LAT, D)


def build_A(ctx_out):
    c = Ctx()
    nc, s = c.nc, c.s
    xin = c.din("xin", [LAT, D])
    cin = c.din("cin", [CTX, D])
    rows = c.din("rows", [3, D])
    colv = c.din("colv", [128, 16])
    win = c.din("win", [D, INW])
    convw = c.din("convw", [128, 8, 5])
    convb = c.din("convb", [128, 8])
    srow = c.din("srow", [176])
    rope = c.din("rope", [128, NT, 2, 64])
    natab = c.din("natab", [128, 8 * 4 * 4 * 64])
    cmask = c.din("cmask", [128, 2])
    tri = c.din("tri", [128, 6, 128])
    o_gqT = c.dout("o_gqT", [128, 3, NTC * 128], BF16)
    o_gv = c.dout("o_gv", [128, NTC, 130], BF16)
    o_yaT = c.dout("o_yaT", [64, 4, NTC * 128], BF16)
    o_yacc = c.dout("o_yacc", [128, NTC, 512])
    o_zs = c.dout("o_zs", [128, NTC, 512])
    o_S = c.dout("o_S", [128, 2, 512])
    o_D = c.dout("o_D", [128, 2, 8])
    o_hc = c.dout("o_hc", [128, 2, 512])
    o_CT = c.dout("o_CT", [128, 2, TOK], BF16)
    o_e2 = c.dout("o_e2", [128, NT, 2, 8])
    NQ = NTC * 128 if ctx_out else TOK

    def hk(a, n):
        return [("hT", t) for t in range(a // 128, (a + n - 1) // 128 + 1)]

    def tok_of(ti):
        return HB + ti * 128 if ti < NT else LAT + (ti - NT) * 128

    try:
      with contextlib.ExitStack() as so:
        _build_A_body(c, s, so, locals())
    except _Stop:
        pass
    c.stack.close()
    return nc


def _build_A_body(c, s, so, L):
    globals_ = L
    (xin, cin, rows, colv, win, convw, convb, srow, rope, natab, cmask, tri, o_gqT, o_gv, o_yaT, o_yacc, o_zs,
     o_S, o_D, o_hc, o_CT, o_e2, NQ, hk, tok_of, ctx_out) = [L[k] for k in (
        "xin", "cin", "rows", "colv", "win", "convw", "convb", "srow", "rope", "natab", "cmask", "tri",
        "o_gqT", "o_gv", "o_yaT", "o_yacc", "o_zs", "o_S", "o_D", "o_hc", "o_CT", "o_e2", "NQ", "hk", "tok_of",
        "ctx_out")]
    if True:
        trif = c.sb(so, "trif", [128, 6, 128])
        idb = c.sb(so, "idb", [128, 128], BF16)
        srw = c.sb(so, "srw", [128, 176])
        arow = c.sb(so, "arow", [128, 16])
        xsT = c.sb(so, "xsT", [128, 4, NTC * 128], BF16)
        BT = c.sb(so, "BT", [128, 2, NTC * 128], BF16)
        CT = c.sb(so, "CT", [128, 2, NTC * 128], BF16)
        dt = c.sb(so, "dt", [128, NTC, 16])
        dta = c.sb(so, "dta", [128, NTC, 16])
        psF = [c.ps(so, f"psF{i}", [128, 512]) for i in range(7)]
        psB = c.ps(so, "psB", [128, 1024], BF16)
        rot = Rot(psF[0:5])
        ID_F = trif[:, 5, :]
        s.dma("sp", trif[:], tri, writes=[trif])
        s.dma("sp", srw[:], srow.partition_broadcast(128), writes=[srw])
        s.dve(lambda e: e.tensor_copy(idb[:], trif[:, 5, :]), [trif], [idb])
        s.act(lambda e: e.activation(arow[:], srw[:, 16:32], AF.Exp), [srw], [arow])
        s.dve(lambda e: e.tensor_scalar(arow[:], arow[:], -1.0, None, ALU.mult), [arow], [arow])

        with contextlib.ExitStack() as sna:
            naK = c.sb(sna, "naK", [128, 2, ALLT], BF16)
            naQ = c.sb(sna, "naQ", [128, 2, NTC * 128], BF16)
            naV = c.sb(sna, "naV", [128, 41, 4, 65], BF16)
            s.pool(lambda e: e.memset(naV[:, :, :, 64:65], 1.0), [], [("naV", i) for i in range(41)])

            with contextlib.ExitStack() as sp_:
                hT = c.sb(sp_, "hT", [128, 8, ALLT], BF16)
                wbuf = [c.sb(sp_, f"wbuf{i}", [128, 8, 1040], BF16) for i in range(2)]
                wst = [c.sb(sp_, f"wst{i}", [128, 1040]) for i in range(3)]
                shv = c.sb(sp_, "shv", [128, 16])
                s.dma("sp", shv[:], colv, writes=[shv])
                shv3 = shv[:].rearrange("p (c w) -> p c w", w=2)

                def load_w(buf, specs, q="sp"):
                    for k in range(8):
                        w = wst[k % 3]
                        for (a, n, d0) in specs:
                            s.dma(q, w[:, d0:d0 + n], win[k * 128:(k + 1) * 128, a:a + n], writes=[w])
                        tot = max(d0 + n for (_, n, d0) in specs)
                        s.pool(lambda e, w=w, k=k, tot=tot: e.tensor_copy(buf[:, k, 0:tot], w[:, 0:tot]),
                               [w], [buf])

                with contextlib.ExitStack() as s1:
                    xt = [c.sb(s1, f"xt{i}", [128, D]) for i in range(2)]
                    xsc = [c.sb(s1, f"xsc{i}", [128, D]) for i in range(2)]
                    srL = c.sb(s1, "srL", [128, D])
                    srC = c.sb(s1, "srC", [128, D])
                    nwr = c.sb(s1, "nwr", [128, D])
                    ss = c.sb(s1, "ss", [128, 24])
                    rs = c.sb(s1, "rs", [128, 24])
                    s.dma("sp", srL[:], rows[0].partition_broadcast(128), writes=[srL])
                    s.dma("sp", srC[:], rows[1].partition_broadcast(128), writes=[srC])
                    s.dma("sp", nwr[:], rows[2].partition_broadcast(128), writes=[nwr])
                    for t_ in (srL, srC):
                        s.dve(lambda e, t_=t_: e.scalar_tensor_tensor(t_[:], t_[:], 1.0, nwr[:], ALU.add, ALU.mult),
                              [t_, nwr], [t_])
                    load_w(wbuf[0], [(0, 768, 0)], q="act")
                    for ti in range(22):
                        b = ti % 2
                        lat = ti < 20
                        src = xin[ti * 128:(ti + 1) * 128, :] if lat else cin[(ti - 20) * 128:(ti - 19) * 128, :]
                        tok = ti * 128
                        sr = srL if lat else srC
                        wh = 0 if lat else 1
                        s.dma("sp", xt[b][:], src, writes=[xt[b]])
                        s.act(lambda e, b=b, ti=ti: e.activation(xsc[b][:], xt[b][:], AF.Square,
                                                                 accum_out=ss[:, ti:ti + 1]),
                              [xt[b]], [xsc[b], ("ss", ti)])
                        s.act(lambda e, ti=ti: e.activation(rs[:, ti:ti + 1], ss[:, ti:ti + 1], AF.Sqrt,
                                                            bias=EPS, scale=1.0 / D), [("ss", ti)], [("rs", ti)])
                        s.dve(lambda e, ti=ti: e.reciprocal(rs[:, ti:ti + 1], rs[:, ti:ti + 1]),
                              [("rs", ti)], [("rs", ti)])
                        s.dve(lambda e, b=b, ti=ti, sr=sr: e.scalar_tensor_tensor(
                            xsc[b][:], xt[b][:], rs[:, ti:ti + 1], sr[:], ALU.mult, ALU.mult),
                            [xt[b], ("rs", ti), sr], [xsc[b]])
                        for half in range(2):
                            p = rot.next()
                            for cc in range(4):
                                ch = half * 4 + cc
                                s.pe(lambda e, p=p, cc=cc, ch=ch, b=b: e.transpose(
                                    p[:, cc * 128:(cc + 1) * 128], xsc[b][:, ch * 128:(ch + 1) * 128], ID_F),
                                    [xsc[b], trif], [p])
                            s.dve(lambda e, p=p, half=half, tok=tok, wh=wh: e.tensor_tensor(
                                hT[:, half * 4:half * 4 + 4, tok:tok + 128],
                                p[:].rearrange("p (c t) -> p c t", c=4),
                                shv3[:, half * 4:half * 4 + 4, wh:wh + 1].to_broadcast([128, 4, 128]), ALU.add),
                                [p, shv], [("hT", ti)])
                    s.flush()
                stop_check(s, 1)

                for ch in range(2):
                    for (cols, dst, key, rngs) in (
                        (C_NAK, naK, "naK", [(a, min(512, ALLT - a), a) for a in range(0, ALLT, 512)]),
                        (C_NAQ, naQ, "naQ", [(HB + a, 512, a) for a in range(0, TOK, 512)] + [(LAT, 256, TOK)]),
                    ):
                        for (a, n, d0) in rngs:
                            p = rot.next()
                            for k in range(8):
                                s.pe(lambda e, p=p, k=k, a=a, n=n, cols=cols, ch=ch: e.matmul(
                                    p[:, 0:n], wbuf[0][:, k, cols + ch * 128:cols + ch * 128 + 128],
                                    hT[:, k, a:a + n], start=(k == 0), stop=(k == 7)),
                                    [wbuf[0]] + hk(a, n), [p])
                            c.copy(c.evac_eng(), dst[:, ch, d0:d0 + n], p[:, 0:n], [p], [(key, ch)])
                for idx in range(41):
                    tok = idx * 64 if idx < 39 else LAT + (idx - 39) * 128
                    p = rot.next()
                    for k in range(8):
                        s.pe(lambda e, p=p, k=k, tok=tok: e.matmul(
                            p[:, 0:256], hT[:, k, tok:tok + 128], wbuf[0][:, k, C_NAV:C_NAV + 256],
                            start=(k == 0), stop=(k == 7)), [wbuf[0]] + hk(tok, 128), [p])
                    c.copy(c.evac_eng(), naV[:, idx, :, 0:64], p[:, 0:256].rearrange("p (h d) -> p h d", h=4),
                           [p], [("naV", idx)])

                stop_check(s, 2)
                load_w(wbuf[1], [(C_Z, 512, 0), (C_DT, 528, 512)])
                with contextlib.ExitStack() as s2:
                    zt = [c.sb(s2, f"zt{i}", [128, 512]) for i in range(2)]
                    gvt = [c.sb(s2, f"gvt{i}", [128, 2, 65], BF16) for i in range(2)]
                    gqt = [c.sb(s2, f"gqt{i}", [128, 3, 128], BF16) for i in range(2)]
                    qf_ = [c.sb(s2, f"qf{i}", [128, 384]) for i in range(2)]
                    sq_ = [c.sb(s2, f"sq{i}", [128, 384]) for i in range(2)]
                    t1_ = [c.sb(s2, f"t1{i}", [128, 384]) for i in range(2)]
                    t2_ = [c.sb(s2, f"t2{i}", [128, 384]) for i in range(2)]
                    qb_ = [c.sb(s2, f"qb{i}", [128, 384], BF16) for i in range(2)]
                    s6_ = [c.sb(s2, f"s6{i}", [128, 6]) for i in range(2)]
                    d16_ = [c.sb(s2, f"d16{i}", [128, 16]) for i in range(2)]
                    qkw = c.sb(s2, "qkw", [128, 384])
                    rp = c.sb(s2, "rp", [128, NT, 2, 64])
                    s.dma("sp", rp[:], rope, writes=[rp])
                    for i in range(2):
                        s.pool(lambda e, i=i: e.memset(gvt[i][:, :, 64:65], 1.0), [], [gvt[i]])
                    for j in range(6):
                        srcc = srw[:, 40:104] if j < 4 else srw[:, 104:168]
                        s.dve(lambda e, j=j, srcc=srcc: e.tensor_copy(qkw[:, j * 64:(j + 1) * 64], srcc), [srw], [qkw])
                    def p2b_Z(ti):
                        b = ti % 2
                        qf, sq, t1, t2, qb, s6, d16 = qf_[b], sq_[b], t1_[b], t2_[b], qb_[b], s6_[b], d16_[b]
                        tok = tok_of(ti)
                        hkk = hk(tok, 128)
                        pz = rot.next()
                        for k in range(8):
                            s.pe(lambda e, p=pz, k=k, tok=tok: e.matmul(
                                p[:, 0:512], hT[:, k, tok:tok + 128], wbuf[1][:, k, 0:512],
                                start=(k == 0), stop=(k == 7)), [wbuf[1]] + hkk, [pz])
                        s.act(lambda e, p=pz, b=b: e.activation(zt[b][:], p[:], AF.Silu), [pz], [zt[b]])
                        s.dma("sp", o_zs[:, ti, :], zt[b][:], reads=[zt[b]], writes=["o_zs"])

                    def p2b_A(ti):
                        b = ti % 2
                        qf, sq, t1, t2, qb, s6, d16 = qf_[b], sq_[b], t1_[b], t2_[b], qb_[b], s6_[b], d16_[b]
                        tok = tok_of(ti)
                        hkk = hk(tok, 128)
                        pg = rot.next()
                        for k in range(8):
                            s.pe(lambda e, p=pg, k=k, tok=tok: e.matmul(
                                p[:, 0:384], hT[:, k, tok:tok + 128], wbuf[1][:, k, 528:912],
                                start=(k == 0), stop=(k == 7)), [wbuf[1]] + hkk, [pg])
                        pv = rot.next()
                        for k in range(8):
                            s.pe(lambda e, p=pv, k=k, tok=tok: e.matmul(
                                p[:, 0:128], hT[:, k, tok:tok + 128], wbuf[1][:, k, 912:1040],
                                start=(k == 0), stop=(k == 7)), [wbuf[1]] + hkk, [pv])
                        for k in range(8):
                            s.pe(lambda e, p=pv, k=k, tok=tok: e.matmul(
                                p[:, 128:144], hT[:, k, tok:tok + 128], wbuf[1][:, k, 512:528],
                                start=(k == 0), stop=(k == 7)), [wbuf[1]] + hkk, [pv])
                        s.act(lambda e, p=pv, b=b: e.copy(gvt[b][:, :, 0:64],
                                                          p[:, 0:128].rearrange("p (g d) -> p g d", g=2)),
                              [pv], [gvt[b]])
                        s.dma("sp", o_gv[:, ti, :], gvt[b][:].rearrange("p g d -> p (g d)"),
                              reads=[gvt[b]], writes=["o_gv"])
                        s.dve(lambda e, p=pv: e.tensor_tensor(d16[:], p[:, 128:144], srw[:, 0:16], ALU.add),
                              [pv, srw], [d16])
                        s.act(lambda e: e.activation(d16[:], d16[:], AF.Exp), [d16], [d16])
                        s.act(lambda e, ti=ti: e.activation(dt[:, ti, :], d16[:], AF.Ln, bias=1.0),
                              [d16], [("dt", ti)])
                        s.dve(lambda e, ti=ti: e.tensor_tensor(dta[:, ti, :], dt[:, ti, :], arow[:], ALU.mult),
                              [("dt", ti), arow], [("dta", ti)])
                        s.act(lambda e, p=pg: e.copy(qf[:], p[:, 0:384]), [pg], [qf])
                        s.dve(lambda e: e.tensor_tensor(sq[:], qf[:], qf[:], ALU.mult), [qf], [sq])
                        s.dve(lambda e: e.reduce_sum(s6[:], sq[:].rearrange("p (h d) -> p h d", h=6), axis=AX.X),
                              [sq], [s6])
                        s.act(lambda e: e.activation(s6[:], s6[:], AF.Ln, bias=EPS, scale=1.0 / 64), [s6], [s6])
                        s.act(lambda e: e.activation(s6[:], s6[:], AF.Exp, scale=-0.5), [s6], [s6])

                    def p2b_B(ti):
                        b = ti % 2
                        qf, sq, t1, t2, qb, s6, d16 = qf_[b], sq_[b], t1_[b], t2_[b], qb_[b], s6_[b], d16_[b]
                        tok = tok_of(ti)
                        hkk = hk(tok, 128)
                        s.dve(lambda e: e.tensor_tensor(
                            t1[:].rearrange("p (h d) -> p h d", h=6), qf[:].rearrange("p (h d) -> p h d", h=6),
                            s6[:].unsqueeze(2).to_broadcast([128, 6, 64]), ALU.mult), [qf, s6], [t1])
                        if ti < NT:
                            s.dve(lambda e: e.tensor_tensor(qf[:], t1[:], qkw[:], ALU.mult), [t1, qkw], [qf])
                            cosb = rp[:, ti, 0:1, :].to_broadcast([128, 6, 64])
                            q3 = qf[:].rearrange("p (h d) -> p h d", h=6)
                            s.dve(lambda e, cosb=cosb, q3=q3: e.tensor_tensor(
                                t1[:].rearrange("p (h d) -> p h d", h=6), q3, cosb, ALU.mult), [qf, rp], [t1])
                            q5 = qf[:].rearrange("p (h a b d) -> p h a b d", h=6, a=2, b=2)
                            t5 = t2[:].rearrange("p (h a b d) -> p h a b d", h=6, a=2, b=2)
                            sn = rp[:, ti, 1, :].rearrange("p (a b d) -> p a b d", a=2, b=2)
                            for bb in range(2):
                                s.dve(lambda e, bb=bb, q5=q5, t5=t5, sn=sn: e.tensor_tensor(
                                    t5[:, :, :, bb, :], q5[:, :, :, 1 - bb, :],
                                    sn[:, :, bb, :].unsqueeze(1).to_broadcast([128, 6, 2, 16]), ALU.mult),
                                    [qf, rp], [t2])
                            s.dve(lambda e: e.tensor_tensor(qb[:], t1[:], t2[:], ALU.add), [t1, t2], [qb])
                        else:
                            s.dve(lambda e: e.tensor_tensor(qb[:], t1[:], qkw[:], ALU.mult), [t1, qkw], [qb])
                        for j in range(3):
                            s.pe(lambda e, j=j: e.transpose(psB[:, j * 128:(j + 1) * 128],
                                                            qb[:, j * 128:(j + 1) * 128], idb[:]),
                                 [qb, idb], [psB])
                        s.act(lambda e, b=b: e.copy(gqt[b][:], psB[:, 0:384].rearrange("p (j t) -> p j t", j=3)),
                              [psB], [gqt[b]])
                        s.dma("sp", o_gqT[:, :, ti * 128:(ti + 1) * 128], gqt[b][:], reads=[gqt[b]], writes=["o_gqT"])

                    for ti in range(NTC):
                        p2b_Z(ti)
                    p2b_A(0)
                    for ti in range(NTC):
                        if ti + 1 < NTC:
                            p2b_A(ti + 1)
                        p2b_B(ti)
                    s.flush()

                stop_check(s, 3)
                load_w(wbuf[0], [(C_XBC, 1024, 0)])
                with contextlib.ExitStack() as s3:
                    xst = [c.sb(s3, f"xst{i}", [128, 2052]) for i in range(2)]
                    xstc = [c.sb(s3, f"xstc{i}", [128, 260]) for i in range(2)]
                    acc = c.sb(s3, "acc", [128, TOK])
                    accc = c.sb(s3, "accc", [128, CTX])
                    cw = c.sb(s3, "cw", [128, 8, 5])
                    cb = c.sb(s3, "cb", [128, 8])
                    cm = c.sb(s3, "cm", [128, 2])
                    s.dma("sp", cw[:], convw, writes=[cw])
                    s.dma("sp", cb[:], convb, writes=[cb])
                    s.dma("sp", cm[:], cmask, writes=[cm])
                    for i in range(2):
                        s.pool(lambda e, i=i: e.memset(xstc[i][:], 0.0), [], [xstc[i]])
                    for ch in range(8):
                        b = ch % 2
                        if ch < 4:
                            dst, dk = xsT[:, ch, :], ("xsT", ch)
                        elif ch < 6:
                            dst, dk = BT[:, ch - 4, :], ("BT", ch - 4)
                        else:
                            dst, dk = CT[:, ch - 6, :], ("CT", ch - 6)
                        rngs = [(254 + a, 512, a) for a in range(0, 2048, 512)] + [(2302, 4, 2048)]
                        for (a, n, d0) in rngs + [(LAT, 256, -1)]:
                            p = rot.next()
                            for k in range(8):
                                s.pe(lambda e, p=p, k=k, a=a, n=n, ch=ch: e.matmul(
                                    p[:, 0:n], wbuf[0][:, k, ch * 128:(ch + 1) * 128], hT[:, k, a:a + n],
                                    start=(k == 0), stop=(k == 7)), [wbuf[0]] + hk(a, n), [p])
                            if d0 >= 0:
                                c.copy(c.evac_eng(), xst[b][:, d0:d0 + n], p[:, 0:n], [p], [xst[b]])
                            else:
                                c.copy(c.evac_eng(), xstc[b][:, 2:258], p[:, 0:n], [p], [xstc[b]])
                        s.dve(lambda e, b=b: e.tensor_scalar(xst[b][:, 0:2], xst[b][:, 0:2], cm[:, 0:1], None, ALU.mult),
                              [xst[b], cm], [xst[b]])
                        s.dve(lambda e, b=b: e.tensor_scalar(xst[b][:, 2050:2052], xst[b][:, 2050:2052], cm[:, 1:2],
                                                             None, ALU.mult), [xst[b], cm], [xst[b]])
                        for (src_, ac, n) in ((xst[b], acc, TOK), (xstc[b], accc, CTX)):
                            s.dve(lambda e, src_=src_, ac=ac, n=n, ch=ch: e.tensor_scalar(
                                ac[:], src_[:, 0:n], cw[:, ch, 0:1], cb[:, ch:ch + 1], ALU.mult, ALU.add),
                                [src_, cw, cb], [ac])
                            for j in range(1, 5):
                                s.dve(lambda e, src_=src_, ac=ac, n=n, ch=ch, j=j: e.scalar_tensor_tensor(
                                    ac[:], src_[:, j:j + n], cw[:, ch, j:j + 1], ac[:], ALU.mult, ALU.add),
                                    [src_, cw, ac], [ac])
                        s.act(lambda e, dst=dst: e.activation(dst[:, 0:TOK], acc[:], AF.Silu), [acc], [dk])
                        s.act(lambda e, dst=dst: e.activation(dst[:, TOK:TOK + CTX], accc[:], AF.Silu), [accc], [dk])
                    s.flush()
            stop_check(s, 4)
            build_A_na(c, s, so, rot, psF, trif, naK, naQ, naV, natab, o_yaT, NQ, ctx_out)
        stop_check(s, 5)
        build_A_ssd(c, s, Rot(psF), psB, trif, idb, srw, xsT, BT, CT, dt, dta,
                    o_yacc, o_S, o_D, o_hc, o_CT, o_e2)


def normalize_heads(c, s, st, acc, n, trif, rot, outap, outkey, tmp):
    rsum, otmp = tmp
    s.act(lambda e: e.copy(rsum[64:65, 0:n], acc[64:65, 0:n]), [acc], [rsum])
    s.dve(lambda e: e.reciprocal(rsum[64:65, 0:n], rsum[64:65, 0:n]), [rsum], [rsum])
    bc = rot.next()
    s.pe(lambda e: e.matmul(bc[0:64, 0:n], trif[64:65, 4, 0:64], rsum[64:65, 0:n], start=True, stop=True),
         [rsum, trif], [bc])
    s.act(lambda e: e.copy(otmp[0:64, 0:n], acc[0:64, 0:n]), [acc], [otmp])
    s.dve(lambda e: e.tensor_tensor(outap, otmp[0:64, 0:n], bc[0:64, 0:n], ALU.mult), [otmp, bc], [outkey])


def build_A_na(c, s, so, rot, psF, trif, naK, naQ, naV, natab, o_yaT, NQ, ctx_out):
    with contextlib.ExitStack() as st:
        nt = c.sb(st, "nt", [128, 8 * 4 * 4 * 64])
        s.dma("sp", nt[:], natab, writes=[nt])
        ntv = nt[:].rearrange("p (a j h q) -> p a j h q", a=8, j=4, h=4)
        pcT = [c.sb(st, f"pcT{i}", [128, 2, NTC * 128], BF16) for i in range(2)]
        sbw = [c.sb(st, f"sbw{i}", [128, 4, 64]) for i in range(3)]
        pw = [c.sb(st, f"pw{i}", [128, 4, 64], BF16) for i in range(3)]
        rsum = c.sb(st, "rsum", [128, 512])
        otmp = c.sb(st, "otmp", [64, 512])
        yst = [c.sb(st, f"yst{i}", [64, 512], BF16) for i in range(2)]
        accs = [psF[5], psF[6]]
        gi = 0
        for h in range(4):
            ch, pb = h // 2, (h % 2) * 64
            pc = pcT[h % 2]
            kq = [("naK", ch), ("naQ", ch)]
            for qc in range(0, NQ, 512):
                n = min(512, NQ - qc)
                for kt in range(2):
                    p = rot.next()
                    s.pe(lambda e, p=p, kt=kt, qc=qc, n=n, ch=ch, pb=pb: e.matmul(
                        p[:, 0:n], naK[pb:pb + 64, ch, LAT + kt * 128:LAT + kt * 128 + 128],
                        naQ[pb:pb + 64, ch, qc:qc + n], start=True, stop=True), kq, [p])
                    s.act(lambda e, p=p, kt=kt, qc=qc, n=n, pc=pc: e.activation(
                        pc[:, kt, qc:qc + n], p[:, 0:n], AF.Exp, scale=0.125), [p], [pc])
            def stage1(lr, h=h, ch=ch, pb=pb, kq=kq):
                b = lr % 3
                pat = 0 if 4 <= lr < 29 else (1 + lr if lr < 4 else 5 + lr - 29)
                p = rot.next()
                for j in range(4):
                    s.pe(lambda e, j=j: e.matmul(
                        p[:, j * 64:(j + 1) * 64], naK[pb:pb + 64, ch, (lr + 2 * j) * 64:(lr + 2 * j) * 64 + 128],
                        naQ[pb:pb + 64, ch, lr * 64:lr * 64 + 64], start=True, stop=True), kq, [p])
                s.dve(lambda e: e.scalar_tensor_tensor(
                    sbw[b][:], p[:, 0:256].rearrange("p (j q) -> p j q", j=4), 0.125, ntv[:, pat, :, h, :],
                    ALU.mult, ALU.add), [p, nt], [sbw[b]])
                s.act(lambda e: e.activation(pw[b][:], sbw[b][:], AF.Exp), [sbw[b]], [pw[b]])

            def stage2(lr, acc, h=h, pc=pc):
                b = lr % 3
                col = (lr % 8) * 64
                for j in range(4):
                    s.pe(lambda e, j=j: e.matmul(
                        acc[0:65, col:col + 64], naV[:, lr + 2 * j, h, :], pw[b][:, j, :],
                        start=(j == 0), stop=False), [("naV", lr + 2 * j), pw[b]], [acc])
                for kt in range(2):
                    s.pe(lambda e, kt=kt: e.matmul(
                        acc[0:65, col:col + 64], naV[:, 39 + kt, h, :], pc[:, kt, lr * 64:lr * 64 + 64],
                        start=False, stop=(kt == 1)), [("naV", 39 + kt), pc], [acc])

            stage1(0)
            stage1(1)
            for lr in range(32):
                if lr + 2 < 32:
                    stage1(lr + 2)
                acc = accs[gi % 2]
                stage2(lr, acc)
                if lr % 8 == 7:
                    y = yst[gi % 2]
                    normalize_heads(c, s, st, acc, 512, trif, rot, y[:, :], y, (rsum, otmp))
                    q0 = (lr - 7) * 64
                    s.dma("sp", o_yaT[:, h, q0:q0 + 512], y[:], reads=[y], writes=["o_yaT"])
                    gi += 1
            if not ctx_out:
                y = yst[gi % 2]
                s.pool(lambda e, y=y: e.memset(y[:, 0:256], 0.0), [], [y])
                s.dma("sp", o_yaT[:, h, TOK:TOK + CTX], y[:, 0:256], reads=[y], writes=["o_yaT"])
                gi += 1
            if ctx_out:
                acc = accs[gi % 2]
                for kt in range(2):
                    s.pe(lambda e, acc=acc, kt=kt, h=h, pc=pc: e.matmul(
                        acc[0:65, 0:256], naV[:, 39 + kt, h, :], pc[:, kt, TOK:TOK + CTX],
                        start=(kt == 0), stop=(kt == 1)), [("naV", 39 + kt), pc], [acc])
                y = yst[gi % 2]
                normalize_heads(c, s, st, acc, 256, trif, rot, y[:, 0:256], y, (rsum, otmp))
                s.dma("sp", o_yaT[:, h, TOK:TOK + CTX], y[:, 0:256], reads=[y], writes=["o_yaT"])
                gi += 1
        s.flush()


def build_A_ssd(c, s, rot, psB, trif, idb, srw, xsT, BT, CT, dt, dta,
                o_yacc, o_S, o_D, o_hc, o_CT, o_e2):
    with contextlib.ExitStack() as st:
        xs_tm = c.sb(st, "xs_tm", [128, NTC, 512], BF16)
        B_tm = c.sb(st, "B_tm", [128, NTC, 256], BF16)
        yacc = c.sb(st, "yacc", [128, NTC, 512])
        e2 = c.sb(st, "e2", [128, NT, 2, 8])
        hl = [c.sb(st, f"hl{i}", [128, 512]) for i in range(4)]
        hlb = [c.sb(st, f"hlb{i}", [128, 512], BF16) for i in range(4)]
        Pc = c.sb(st, "Pc", [128, 2, 8])
        ex = [[c.sb(st, f"ex{d}{p}", [128, 24]) for p in range(2)] for d in range(2)]
        wts = [[c.sb(st, f"wts{d}{p}", [128, 8]) for p in range(2)] for d in range(2)]
        xw = [[c.sb(st, f"xw{d}{p}", [128, 512], BF16) for p in range(2)] for d in range(2)]
        cbm = [[c.sb(st, f"cbm{d}{p}", [128, 2, 128]) for p in range(2)] for d in range(2)]
        rh_ = [[c.sb(st, f"rh{d}_{i}", [128, 128]) for i in range(4)] for d in range(2)]
        E_ = [[[c.sb(st, f"E{d}{p}_{i}", [128, 512]) for i in range(2)] for p in range(2)] for d in range(2)]
        gt_ = [[c.sb(st, f"gt{d}_{i}", [128, 128], BF16) for i in range(8)] for d in range(2)]
        tmp = [c.sb(st, f"tmp{i}", [128, 512]) for i in range(2)]
        for i in range(4):
            s.pool(lambda e, i=i: e.memset(hl[i][:], 0.0), [], [hl[i]])
            s.pool(lambda e, i=i: e.memset(hlb[i][:], 0.0), [], [hlb[i]])
        s.pool(lambda e: e.memset(Pc[:], 1.0), [], [Pc])
        s.pool(lambda e: e.memset(yacc[:], 0.0), [], [("yacc", t) for t in range(NTC)])
        if "SSD_LIMIT" in os.environ:
            s.pool(lambda e: e.memset(e2[:], 0.0), [], [e2])
        for ti in range(NTC):
            tok = ti * 128
            for ch in range(4):
                s.pe(lambda e, ch=ch, tok=tok: e.transpose(psB[:, ch * 128:(ch + 1) * 128],
                                                           xsT[:, ch, tok:tok + 128], idb[:]),
                     [("xsT", ch), idb], [psB])
            for g in range(2):
                s.pe(lambda e, g=g, tok=tok: e.transpose(psB[:, 512 + g * 128:512 + (g + 1) * 128],
                                                         BT[:, g, tok:tok + 128], idb[:]),
                     [("BT", g), idb], [psB])
            s.act(lambda e, ti=ti: e.copy(xs_tm[:, ti, :], psB[:, 0:512]), [psB], [("xs_tm", ti)])
            s.dve(lambda e, ti=ti: e.tensor_copy(B_tm[:, ti, :], psB[:, 512:768]), [psB], [("B_tm", ti)])
        def stage1(d, ti, p):
            d0 = d * 8
            L_te, L_oe = (0, 1) if d == 0 else (2, 3)
            tok = ti * 128
            dk = ("dta", ti)
            exb, wtb, xwb, cbb, Eb, rh = ex[d][p], wts[d][p], xw[d][p], cbm[d][p], E_[d][p], rh_[d]
            pss = rot.next()
            for j, L in enumerate((L_te, L_oe, 4)):
                s.pe(lambda e, j=j, L=L: e.matmul(
                    pss[:, j * 8:(j + 1) * 8], trif[:, L, :], dta[:, ti, d0:d0 + 8], start=True, stop=True),
                    [trif, dk], [pss])
            s.act(lambda e: e.activation(exb[:], pss[:, 0:24], AF.Exp), [pss], [exb])
            s.dve(lambda e: e.tensor_tensor(wtb[:], dt[:, ti, d0:d0 + 8], exb[:, 0:8], ALU.mult),
                  [("dt", ti), exb], [wtb])
            s.dve(lambda e: e.tensor_tensor(
                xwb[:].rearrange("p (h d) -> p h d", h=8), xs_tm[:, ti, :].rearrange("p (h d) -> p h d", h=8),
                wtb[:].unsqueeze(2).to_broadcast([128, 8, 64]), ALU.mult), [("xs_tm", ti), wtb], [xwb])
            pcb = rot.next()
            for g in range(2):
                s.pe(lambda e, g=g: e.matmul(
                    pcb[:, g * 128:(g + 1) * 128], BT[:, g, tok:tok + 128], CT[:, g, tok:tok + 128],
                    start=True, stop=True), [("BT", g), ("CT", g)], [pcb])
            s.dve(lambda e: e.tensor_tensor(
                cbb[:], pcb[:, 0:256].rearrange("p (g t) -> p g t", g=2),
                trif[:, L_oe:L_oe + 1, :].to_broadcast([128, 2, 128]), ALU.mult), [pcb, trif], [cbb])
            for half in range(2):
                pseg = rot.next()
                for hh in range(4):
                    h = half * 4 + hh
                    r_ = rh[hh]
                    s.act(lambda e, r_=r_, h=h: e.mul(r_[:], trif[:, L_oe, :], dta[:, ti, d0 + h:d0 + h + 1]),
                          [trif, dk], [r_])
                    s.pe(lambda e, pseg=pseg, hh=hh, r_=r_: e.matmul(
                        pseg[:, hh * 128:(hh + 1) * 128], trif[:, L_te, :], r_[:], start=True, stop=True),
                        [trif, r_], [pseg])
                Eh = Eb[half]
                s.act(lambda e, pseg=pseg, Eh=Eh: e.activation(Eh[:], pseg[:], AF.Exp), [pseg], [Eh])

        def stage2(d, ti, si, p):
            d0 = d * 8
            tok = ti * 128
            lat = ti < NT
            exb, xwb, cbb, Eb, gt = ex[d][p], xw[d][p], cbm[d][p], E_[d][p], gt_[d]
            oe, dec = exb[:, 8:16], exb[:, 16:24]
            b = d
            pst = rot.next()
            for g in range(2):
                s.pe(lambda e, g=g: e.matmul(
                    pst[:, g * 256:(g + 1) * 256], B_tm[:, ti, g * 128:(g + 1) * 128],
                    xwb[:, g * 256:(g + 1) * 256], start=True, stop=True), [("B_tm", ti), xwb], [pst])
            pyo = rot.next()
            for g in range(2):
                s.pe(lambda e, g=g: e.matmul(
                    pyo[:, g * 256:(g + 1) * 256], CT[:, g, tok:tok + 128], hlb[si][:, g * 256:(g + 1) * 256],
                    start=True, stop=True), [("CT", g), hlb[si]], [pyo])
            hs = hl[si]
            s.dve(lambda e: e.tensor_tensor(
                hs[:].rearrange("p (h d) -> p h d", h=8), hs[:].rearrange("p (h d) -> p h d", h=8),
                dec.unsqueeze(2).to_broadcast([128, 8, 64]), ALU.mult), [hs, exb], [hs])
            s.dve(lambda e: e.tensor_tensor(hs[:], hs[:], pst[:], ALU.add), [hs, pst], [hs])
            s.act(lambda e: e.copy(hlb[si][:], hs[:]), [hs], [hlb[si]])
            for h in range(8):
                Eh = Eb[h // 4]
                hh = h % 4
                s.dve(lambda e, Eh=Eh, hh=hh, h=h: e.scalar_tensor_tensor(
                    gt[h][:], Eh[:, hh * 128:(hh + 1) * 128], dt[:, ti, d0 + h:d0 + h + 1],
                    cbb[:, h // 4, :], ALU.mult, ALU.mult), [Eh, ("dt", ti), cbb], [gt[h]])
            pyd = rot.next()
            for h in range(8):
                s.pe(lambda e, h=h: e.matmul(
                    pyd[:, h * 64:(h + 1) * 64], gt[h][:], xs_tm[:, ti, h * 64:(h + 1) * 64],
                    start=True, stop=True), [gt[h], ("xs_tm", ti)], [pyd])
            yk = ("yacc", ti)
            s.dve(lambda e: e.tensor_tensor(yacc[:, ti, :], yacc[:, ti, :], pyd[:], ALU.add), [pyd, yk], [yk])
            if d == 0:
                s.dve(lambda e: e.tensor_tensor(
                    tmp[b][:].rearrange("p (h d) -> p h d", h=8),
                    xs_tm[:, ti, :].rearrange("p (h d) -> p h d", h=8),
                    srw[:, 32:40].unsqueeze(2).to_broadcast([128, 8, 64]), ALU.mult),
                    [("xs_tm", ti), srw], [tmp[b]])
                s.pool(lambda e: e.tensor_tensor(yacc[:, ti, :], yacc[:, ti, :], tmp[b][:], ALU.add),
                       [yk, tmp[b]], [yk])
            s.dve(lambda e: e.tensor_tensor(
                tmp[b][:].rearrange("p (h d) -> p h d", h=8), pyo[:].rearrange("p (h d) -> p h d", h=8),
                oe.unsqueeze(2).to_broadcast([128, 8, 64]), ALU.mult), [pyo, exb], [tmp[b]])
            s.pool(lambda e: e.tensor_tensor(yacc[:, ti, :], yacc[:, ti, :], tmp[b][:], ALU.add),
                   [yk, tmp[b]], [yk])
            if lat:
                s.dve(lambda e: e.tensor_tensor(e2[:, ti, d, :], oe, Pc[:, d, :], ALU.mult), [exb, Pc], [e2])
                s.dve(lambda e: e.tensor_tensor(Pc[:, d, :], Pc[:, d, :], dec, ALU.mult), [exb, Pc], [Pc])

        orders = []
        for d in range(2):
            order = [(t, 2 + d) for t in ((16, 17) if d == 0 else (17, 16))]
            order += [(t, d) for t in (range(NT) if d == 0 else range(NT - 1, -1, -1))]
            orders.append(order)
        nstep = min(NTC, int(os.environ.get("SSD_LIMIT", "1000")))
        for d in range(2):
            stage1(d, orders[d][0][0], 0)
        for step in range(nstep):
            if step + 1 < nstep:
                for d in range(2):
                    stage1(d, orders[d][step + 1][0], (step + 1) % 2)
            for d in range(2):
                ti, si = orders[d][step]
                stage2(d, ti, si, step % 2)
        s.dma("sp", o_yacc, yacc[:], reads=[("yacc", t) for t in range(NTC)], writes=["o_yacc"])
        for d in range(2):
            s.dma("sp", o_S[:, d, :], hl[d][:], reads=[hl[d]], writes=["o_S"])
            s.dma("sp", o_hc[:, d, :], hl[2 + d][:], reads=[hl[2 + d]], writes=["o_hc"])
        s.dma("sp", o_D, Pc[:], reads=[Pc], writes=["o_D"])
        s.dma("sp", o_CT, CT[:, :, 0:TOK], reads=[("CT", 0), ("CT", 1)], writes=["o_CT"])
        s.dma("sp", o_e2, e2[:], reads=[e2], writes=["o_e2"])
        s.flush()


def mod_parts(mod, layer, which):
    m = mod[layer, which]
    return [m[i * D:(i + 1) * D] for i in range(6)]


def col_layout(v):
    return np.ascontiguousarray(v.reshape(8, 128).T)


GQ_PERM = np.concatenate([np.arange(0, 64), np.arange(128, 192), np.arange(64, 128), np.arange(192, 256)])


def run_A(inp, layer, mod, xfull, cx, ctx_out):
    sh_m, sc_m, _, _, _, _ = mod_parts(mod, layer, 0)
    csh_m, csc_m, _, _, _, _ = mod_parts(mod, layer, 1)
    rows = np.ascontiguousarray(np.stack([sc_m, csc_m, inp["norm_attn_w"][layer]]).astype(np.float32))
    colv = np.ascontiguousarray(np.stack([col_layout(sh_m), col_layout(csh_m)], axis=-1).reshape(128, 16))
    win = inp["w_in"][layer].copy()
    win[:, C_GQ:C_GQ + 256] = inp["w_in"][layer][:, C_GQ + GQ_PERM]
    convw = np.ascontiguousarray(inp["ssd_conv_w"][layer].reshape(5, 8, 128).transpose(2, 1, 0))
    convb = col_layout(inp["ssd_conv_b"][layer])
    srow = np.zeros(176, np.float32)
    srow[0:16] = inp["ssd_dt_bias"][layer].reshape(16)
    srow[16:32] = inp["ssd_a_log"][layer].reshape(16)
    srow[32:40] = inp["ssd_d"][layer]
    srow[40:104] = inp["q_norm_w"][layer]
    srow[104:168] = inp["k_norm_w"][layer]
    tri = tri_consts()
    maps = []
    for k in range(NCORE):
        cm = np.zeros((128, 2), np.float32)
        cm[:, 0] = 1.0 if k > 0 else 0.0
        cm[:, 1] = 1.0 if k < NCORE - 1 else 0.0
        maps.append(dict(
            xin=x_with_halo(xfull, k), cin=np.ascontiguousarray(cx), rows=rows, colv=colv, win=win,
            convw=convw, convb=convb, srow=srow, rope=rope_tables(k),
            natab=np.ascontiguousarray(na_bias_table(inp["na_rpb"][layer], k).reshape(128, -1)),
            cmask=cm, tri=tri))
    res = run_bass_kernel_spmd(build_A(ctx_out), maps, core_ids=list(range(NCORE)))
    return res.results


def build_B(ctx_out, last):
    c = Ctx()
    nc, s = c.nc, c.s
    NTT = NTC if ctx_out else NT
    NQ = NTT * 128
    xown = c.din("xown", [NTC * 128, D])
    yacc_d = c.din("yacc_d", [128, NTC, 512])
    zs_d = c.din("zs_d", [128, NTC, 512])
    yaT_d = c.din("yaT_d", [64, 4, NTC * 128], BF16)
    CT_d = c.din("CT_d", [128, 2, TOK], BF16)
    e2_d = c.din("e2_d", [128, NT, 2, 8])
    S_all = c.din("S_all", [NCORE, 128, 2, 512])
    D_all = c.din("D_all", [NCORE, 128, 2, 8])
    hc_d = c.din("hc_d", [128, 2, 512])
    sel_d = c.din("sel_d", [128, 8])
    gqT_d = c.din("gqT_d", [128, 2, NTC * 128], BF16)
    kt_all = c.din("kt_all", [128, SEQ + CTX], BF16)
    v_all = c.din("v_all", [128, 130, 130], BF16)
    wout = c.din("wout", [D, D])
    wgate = c.din("wgate", [D, FFN])
    wup = c.din("wup", [D, FFN])
    wdown = c.din("wdown", [FFN, D])
    rowsB = c.din("rowsB", [8, D])
    ssdnw = c.din("ssdnw", [512])
    colvB = c.din("colvB", [128, 16])
    tri = c.din("tri", [128, 6, 128])
    xo = c.dout("xo", [NTT * 128, D])
    ybT_d = nc.dram_tensor("ybT_s", [128, 4, NTC * 128], BF16).ap()
    ygT_d = nc.dram_tensor("ygT_s", [64, 4, NTC * 128], BF16).ap()

    with contextlib.ExitStack() as so:
        trif = c.sb(so, "trif", [128, 6, 128])
        idb = c.sb(so, "idb", [128, 128], BF16)
        psF = [c.ps(so, f"psF{i}", [128, 512]) for i in range(7)]
        psB = c.ps(so, "psB", [128, 1024], BF16)
        rot = Rot(psF[0:5])
        s.dma("sp", trif[:], tri, writes=[trif])
        s.dve(lambda e: e.tensor_copy(idb[:], trif[:, 5, :]), [trif], [idb])
        ID_F = trif[:, 5, :]

        with contextlib.ExitStack() as st:
            KT = c.sb(st, "KT", [128, SEQ + CTX], BF16)
            V = c.sb(st, "V", [128, 130, 130], BF16)
            QT = c.sb(st, "QT", [128, 2, NTC * 128], BF16)
            s.dma("pool", QT[:], gqT_d, writes=[QT])
            for i in range(4):
                a, n = i * 4160, 4160
                s.dma("pool", KT[:, a:a + n], kt_all[:, a:a + n], writes=[("KT", i)])
            for i in range(5):
                s.dma("pool", V[:, i * 26:(i + 1) * 26, :], v_all[:, i * 26:(i + 1) * 26, :], writes=[("V", i)])
            yacc = c.sb(st, "yacc", [128, NTC, 512])
            CT = c.sb(st, "CT", [128, 2, TOK], BF16)
            e2 = c.sb(st, "e2", [128, NT, 2, 8])
            hc = c.sb(st, "hc", [128, 2, 512])
            sel = c.sb(st, "sel", [128, 8])
            P = c.sb(st, "P", [128, 512])
            Hin = c.sb(st, "Hin", [128, 512])
            Hinb = c.sb(st, "Hinb", [128, 2, 512], BF16)
            Sj = [c.sb(st, f"Sj{i}", [128, 512]) for i in range(2)]
            Dj = [c.sb(st, f"Dj{i}", [128, 8]) for i in range(2)]
            tmp = [c.sb(st, f"tmpb{i}", [128, 512]) for i in range(2)]
            zt = [c.sb(st, f"ztb{i}", [128, 512]) for i in range(2)]
            gz = [c.sb(st, f"gz{i}", [128, 512]) for i in range(2)]
            ynb = [c.sb(st, f"ynb{i}", [128, 512], BF16) for i in range(2)]
            ybs = [c.sb(st, f"ybs{i}", [128, 4, 128], BF16) for i in range(2)]
            nwr = c.sb(st, "nwr5", [128, 512])
            ss = c.sb(st, "ssb", [128, NTC])
            rs = c.sb(st, "rsb", [128, NTC])
            s.dma("sp", yacc[:], yacc_d, writes=[("yacc", t) for t in range(NTC)])
            s.dma("sp", CT[:], CT_d, writes=[CT])
            s.dma("sp", e2[:], e2_d, writes=[e2])
            s.dma("sp", hc[:], hc_d, writes=[hc])
            s.dma("sp", sel[:], sel_d, writes=[sel])
            s.dma("sp", nwr[:], ssdnw.partition_broadcast(128), writes=[nwr])

            def hin_dir(d):
                js = list(range(0, 7)) if d == 0 else list(range(7, 0, -1))
                first_sel = 0 if d == 0 else 7
                s.dve(lambda e: e.tensor_copy(P[:], hc[:, d, :]), [hc], [P])
                s.dve(lambda e: e.tensor_scalar(Hin[:], P[:], sel[:, first_sel:first_sel + 1], None, ALU.mult),
                      [P, sel], [Hin])
                for i, j in enumerate(js):
                    b = i % 2
                    nxt = j + 1 if d == 0 else j - 1
                    s.dma("sp", Sj[b][:], S_all[j, :, d, :], writes=[Sj[b]])
                    s.dma("sp", Dj[b][:], D_all[j, :, d, :], writes=[Dj[b]])
                    s.dve(lambda e, b=b: e.tensor_tensor(
                        P[:].rearrange("p (h d) -> p h d", h=8), P[:].rearrange("p (h d) -> p h d", h=8),
                        Dj[b][:].unsqueeze(2).to_broadcast([128, 8, 64]), ALU.mult), [P, Dj[b]], [P])
                    s.dve(lambda e, b=b: e.tensor_tensor(P[:], P[:], Sj[b][:], ALU.add), [P, Sj[b]], [P])
                    s.dve(lambda e, nxt=nxt: e.scalar_tensor_tensor(
                        Hin[:], P[:], sel[:, nxt:nxt + 1], Hin[:], ALU.mult, ALU.add), [P, sel, Hin], [Hin])
                s.pool(lambda e: e.tensor_copy(Hinb[:, d, :], Hin[:]), [Hin], [("Hinb", d)])

            for d in range(2):
                hin_dir(d)
            itc = [0]

            def b1_corr(ti):
                for d in range(2):
                    b = itc[0] % 2
                    itc[0] += 1
                    pc = psF[4]
                    for g in range(2):
                        s.pe(lambda e, p=pc, g=g, ti=ti, d=d: e.matmul(
                            p[:, g * 256:(g + 1) * 256], CT[:, g, ti * 128:(ti + 1) * 128],
                            Hinb[:, d, g * 256:(g + 1) * 256], start=True, stop=True), [CT, ("Hinb", d)], [pc])
                    s.dve(lambda e, p=pc, b=b, ti=ti, d=d: e.tensor_tensor(
                        tmp[b][:].rearrange("p (h d) -> p h d", h=8), p[:].rearrange("p (h d) -> p h d", h=8),
                        e2[:, ti, d, :].unsqueeze(2).to_broadcast([128, 8, 64]), ALU.mult), [pc, e2], [tmp[b]])
                    s.pool(lambda e, b=b, ti=ti: e.tensor_tensor(yacc[:, ti, :], yacc[:, ti, :], tmp[b][:], ALU.add),
                           [("yacc", ti), tmp[b]], [("yacc", ti)])
            def b1_gate(ti):
                b = ti % 2
                s.dma("sp", zt[b][:], zs_d[:, ti, :], writes=[zt[b]])
                s.dve(lambda e, b=b, ti=ti: e.tensor_tensor(gz[b][:], yacc[:, ti, :], zt[b][:], ALU.mult),
                      [("yacc", ti), zt[b]], [gz[b]])
                s.act(lambda e, b=b, ti=ti: e.activation(zt[b][:], gz[b][:], AF.Square, accum_out=ss[:, ti:ti + 1]),
                      [gz[b]], [zt[b], ("ssb", ti)])
                s.act(lambda e, ti=ti: e.activation(rs[:, ti:ti + 1], ss[:, ti:ti + 1], AF.Sqrt, bias=EPS,
                                                    scale=1.0 / 512), [("ssb", ti)], [("rsb", ti)])
                s.dve(lambda e, ti=ti: e.reciprocal(rs[:, ti:ti + 1], rs[:, ti:ti + 1]), [("rsb", ti)], [("rsb", ti)])
                s.dve(lambda e, b=b, ti=ti: e.scalar_tensor_tensor(
                    ynb[b][:], gz[b][:], rs[:, ti:ti + 1], nwr[:], ALU.mult, ALU.mult),
                    [gz[b], ("rsb", ti), nwr], [ynb[b]])
                for ch in range(4):
                    s.pe(lambda e, ch=ch, b=b: e.transpose(psB[:, ch * 128:(ch + 1) * 128],
                                                           ynb[b][:, ch * 128:(ch + 1) * 128], idb[:]),
                         [ynb[b], idb], [psB])
                s.dve(lambda e, b=b: e.tensor_copy(ybs[b][:], psB[:, 0:512].rearrange("p (c t) -> p c t", c=4)),
                      [psB], [ybs[b]])
                s.dma("sp", ybT_d[:, :, ti * 128:(ti + 1) * 128], ybs[b][:], reads=[ybs[b]], writes=["ybT_s"])

            b1_todo = list(range(NTT))

            def b1_some(k):
                for _ in range(k):
                    if b1_todo:
                        ti = b1_todo.pop(0)
                        if ti < NT:
                            b1_corr(ti)
                        b1_gate(ti)

            if True:
                PA = [c.sb(st, f"PA{i}", [128, 512], BF16) for i in range(3)]
                PB = [c.sb(st, f"PB{i}", [128, 512], BF16) for i in range(3)]
                rsum = c.sb(st, "rsum", [128, 512])
                otmp = c.sb(st, "otmp", [64, 512])
                yst = [c.sb(st, f"ystg{i}", [64, 512], BF16) for i in range(2)]
                gi = 0
                chunks = [(a, 512, list(range(130))) for a in range(0, TOK, 512)]
                if ctx_out:
                    chunks.append((TOK, 256, [128, 129]))
                rot4 = Rot(psF[0:4])
                accA, accB = psF[5], psF[6]
                for (qc, n, kts) in chunks:
                    for pair in range(2):
                        L = len(kts)

                        def qk(ki, qc=qc, n=n, kts=kts, pair=pair):
                            kt = kts[ki]
                            b = ki % 3
                            pA = rot4.next()
                            pB = rot4.next()
                            kk = ("KT", (kt * 128) // 4160)
                            kk2 = ("KT", (kt * 128 + 127) // 4160)
                            s.pe(lambda e: e.matmul(
                                pA[:, 0:n], KT[0:64, kt * 128:(kt + 1) * 128], QT[0:64, pair, qc:qc + n],
                                start=True, stop=True), [kk, kk2, QT], [pA])
                            s.pe(lambda e: e.matmul(
                                pB[:, 0:n], KT[64:128, kt * 128:(kt + 1) * 128], QT[64:128, pair, qc:qc + n],
                                start=True, stop=True), [kk, kk2, QT], [pB])
                            s.act(lambda e: e.activation(PA[b][:, 0:n], pA[:, 0:n], AF.Exp, scale=0.125),
                                  [pA], [PA[b]])
                            s.act(lambda e: e.activation(PB[b][:, 0:n], pB[:, 0:n], AF.Exp, scale=0.125),
                                  [pB], [PB[b]])

                        def pv(ki, n=n, kts=kts, L=L):
                            kt = kts[ki]
                            b = ki % 3
                            vk = ("V", kt // 26)
                            s.pe(lambda e: e.matmul(
                                accA[0:65, 0:n], V[:, kt, 0:65], PA[b][:, 0:n], start=(ki == 0), stop=(ki == L - 1)),
                                [vk, PA[b]], [accA])
                            s.pe(lambda e: e.matmul(
                                accB[0:65, 0:n], V[:, kt, 65:130], PB[b][:, 0:n], start=(ki == 0), stop=(ki == L - 1)),
                                [vk, PB[b]], [accB])

                        qk(0)
                        for ki in range(L):
                            if ki + 1 < L:
                                qk(ki + 1)
                            pv(ki)
                        for (acc, head) in ((accA, pair), (accB, 2 + pair)):
                            y = yst[gi % 2]
                            gi += 1
                            normalize_heads(c, s, st, acc, n, trif, rot4, y[:, 0:n], y, (rsum, otmp))
                            s.dma("sp", ygT_d[:, head, qc:qc + n], y[:, 0:n], reads=[y], writes=["ygT_s"])
                        b1_some(2)
                b1_some(NTC)
                s.flush()

        xres = c.sb(so, "xres", [128, NTC, D])
        for ti in range(NTT):
            s.dma("sp", xres[:, ti, :], xown[ti * 128:(ti + 1) * 128, :], writes=[("x", ti)])
        build_B_wout(c, s, rot, rowsB, wout, yaT_d, ybT_d, ygT_d, xres, NTT)
        build_B_ffn(c, s, rot, trif, rowsB, colvB, wgate, wup, wdown, xres, NTT, xo, last)
    c.stack.close()
    return nc


def build_B_wout(c, s, rot, rowsB, wout, yaT_d, ybT_d, ygT_d, xres, NTT):
    with contextlib.ExitStack() as st:
        yaT = c.sb(st, "yaT", [128, 2, NTC * 128], BF16)
        ybT = c.sb(st, "ybT", [128, 4, NTC * 128], BF16)
        ygT = c.sb(st, "ygT", [128, 2, NTC * 128], BF16)
        nv = 2 if NTT > NT else 1
        wo = [c.sb(st, f"wo{i}", [128, 8, D], BF16) for i in range(nv)]
        gm = [c.sb(st, f"gm{i}", [128, D]) for i in range(nv)]
        wstS = [c.sb(st, f"wstS{i}", [128, D]) for i in range(3)]
        for v in range(nv):
            s.dma("sp", gm[v][:], rowsB[v].partition_broadcast(128), writes=[gm[v]])
        for i in range(8):
            w = wstS[i % 3]
            s.dma("sp" if i % 2 == 0 else "pool", w[:], wout[i * 128:(i + 1) * 128, :], writes=[w])
            for v in range(nv):
                s.pool(lambda e, w=w, v=v, i=i: e.tensor_tensor(wo[v][:, i, :], w[:], gm[v][:], ALU.mult),
                       [w, gm[v]], [wo[v]])
        for j in range(2):
            for hh in range(2):
                s.dma("sp", yaT[hh * 64:(hh + 1) * 64, j, 0:NTT * 128], yaT_d[:, 2 * j + hh, 0:NTT * 128],
                      writes=[yaT])
                s.dma("sp", ygT[hh * 64:(hh + 1) * 64, j, 0:NTT * 128], ygT_d[:, 2 * j + hh, 0:NTT * 128],
                      reads=["ygT_s"], writes=[ygT])
        s.dma("sp", ybT[:, :, 0:NTT * 128], ybT_d[:, :, 0:NTT * 128], reads=["ybT_s"], writes=[ybT])
        for ti in range(NTT):
            v = 0 if ti < NT else 1
            tok = ti * 128
            for half in range(2):
                p = rot.next()
                cs = slice(half * 512, (half + 1) * 512)
                mm = []
                for j in range(2):
                    mm.append((yaT[:, j, tok:tok + 128], wo[v][:, j, cs], [yaT, wo[v]]))
                for ch in range(4):
                    mm.append((ybT[:, ch, tok:tok + 128], wo[v][:, 2 + ch, cs], [ybT, wo[v]]))
                for j in range(2):
                    mm.append((ygT[:, j, tok:tok + 128], wo[v][:, 6 + j, cs], [ygT, wo[v]]))
                for i, (l, r, rd) in enumerate(mm):
                    s.pe(lambda e, p=p, l=l, r=r, i=i: e.matmul(p[:], l, r, start=(i == 0), stop=(i == 7)), rd, [p])
                s.dve(lambda e, p=p, ti=ti, cs=cs: e.tensor_tensor(xres[:, ti, cs], xres[:, ti, cs], p[:], ALU.add),
                      [p, ("x", ti)], [("x", ti)])
        s.flush()


def build_B_ffn(c, s, rot, trif, rowsB, colvB, wgate, wup, wdown, xres, NTT, xo, last):
    G = 4
    nv = 2 if NTT > NT else 1
    NTOK = NTT * 128
    ID_F = trif[:, 5, :]
    with contextlib.ExitStack() as st:
        hfT = c.sb(st, "hfT", [128, 8, NTC * 128], BF16)
        shv = c.sb(st, "shvB", [128, 16])
        s.dma("sp", shv[:], colvB, writes=[shv])
        shv3 = shv[:].rearrange("p (c w) -> p c w", w=2)
        wg = [c.sb(st, f"wg{i}", [128, 8, G * 128], BF16) for i in range(2)]
        wu = [c.sb(st, f"wu{i}", [128, 8, G * 128], BF16) for i in range(2)]
        wsg = [c.sb(st, f"wsg{i}", [128, G * 128]) for i in range(3)]
        passes = [(a, min(G, NCH - a)) for a in range(0, NCH, G)]
        wi = [0]

        def load_gu(pi):
            c0, g = passes[pi]
            pb = pi % 2
            for (src, dst) in ((wgate, wg[pb]), (wup, wu[pb])):
                for k in range(8):
                    w = wsg[wi[0] % 3]
                    wi[0] += 1
                    s.dma("sp", w[:, 0:g * 128], src[k * 128:(k + 1) * 128, c0 * 128:(c0 + g) * 128], writes=[w])
                    s.pool(lambda e, w=w, dst=dst, k=k, g=g: e.tensor_copy(dst[:, k, 0:g * 128], w[:, 0:g * 128]),
                           [w], [dst])

        load_gu(0)
        with contextlib.ExitStack() as s1:
            xsc = [c.sb(s1, f"xscB{i}", [128, D]) for i in range(2)]
            sr = [c.sb(s1, f"srB{i}", [128, D]) for i in range(nv)]
            nwr = c.sb(s1, "nwrB", [128, D])
            ss = c.sb(s1, "ssf", [128, NTC])
            rs = c.sb(s1, "rsf", [128, NTC])
            s.dma("sp", nwr[:], rowsB[6].partition_broadcast(128), writes=[nwr])
            for v in range(nv):
                s.dma("sp", sr[v][:], rowsB[4 + v].partition_broadcast(128), writes=[sr[v]])
                s.dve(lambda e, v=v: e.scalar_tensor_tensor(sr[v][:], sr[v][:], 1.0, nwr[:], ALU.add, ALU.mult),
                      [sr[v], nwr], [sr[v]])
            for ti in range(NTT):
                b = ti % 2
                v = 0 if ti < NT else 1
                tok = ti * 128
                s.act(lambda e, b=b, ti=ti: e.activation(xsc[b][:], xres[:, ti, :], AF.Square,
                                                         accum_out=ss[:, ti:ti + 1]), [("x", ti)], [xsc[b], ("ssf", ti)])
                s.act(lambda e, ti=ti: e.activation(rs[:, ti:ti + 1], ss[:, ti:ti + 1], AF.Sqrt, bias=EPS,
                                                    scale=1.0 / D), [("ssf", ti)], [("rsf", ti)])
                s.dve(lambda e, ti=ti: e.reciprocal(rs[:, ti:ti + 1], rs[:, ti:ti + 1]), [("rsf", ti)], [("rsf", ti)])
                s.dve(lambda e, b=b, ti=ti, v=v: e.scalar_tensor_tensor(
                    xsc[b][:], xres[:, ti, :], rs[:, ti:ti + 1], sr[v][:], ALU.mult, ALU.mult),
                    [("x", ti), ("rsf", ti), sr[v]], [xsc[b]])
                for half in range(2):
                    p = rot.next()
                    for cc in range(4):
                        ch = half * 4 + cc
                        s.pe(lambda e, p=p, cc=cc, ch=ch, b=b: e.transpose(
                            p[:, cc * 128:(cc + 1) * 128], xsc[b][:, ch * 128:(ch + 1) * 128], ID_F),
                            [xsc[b], trif], [p])
                    s.dve(lambda e, p=p, half=half, tok=tok, v=v: e.tensor_tensor(
                        hfT[:, half * 4:half * 4 + 4, tok:tok + 128], p[:].rearrange("p (c t) -> p c t", c=4),
                        shv3[:, half * 4:half * 4 + 4, v:v + 1].to_broadcast([128, 4, 128]), ALU.add),
                        [p, shv], [("hfT", ti)])
            s.flush()
        with contextlib.ExitStack() as s2:
            nwd = 2 if nv == 1 else 1
            wd_ = [[c.sb(s2, f"wd{i}_{j}", [128, G, D], BF16) for i in range(nv)] for j in range(nwd)]
            gf = [c.sb(s2, f"gf{i}", [128, D]) for i in range(nv)]
            wsd = [c.sb(s2, f"wsd{i}", [128, D]) for i in range(2)]
            actT = [c.sb(s2, f"actT{i}", [128, G, 512], BF16) for i in range(2)]
            sg = [c.sb(s2, f"sg{i}", [128, 512]) for i in range(2)]
            for v in range(nv):
                s.dma("sp", gf[v][:], rowsB[2 + v].partition_broadcast(128), writes=[gf[v]])
            tchunks = [(a, min(512, NTOK - a)) for a in range(0, NTOK, 512)]
            for pi, (c0, g) in enumerate(passes):
                pb = pi % 2
                wd = wd_[pi % nwd]
                if pi > 0:
                    load_gu(pi)
                for j in range(g):
                    w = wsd[j % 2]
                    s.dma("sp", w[:], wdown[(c0 + j) * 128:(c0 + j + 1) * 128, :], writes=[w])
                    for v in range(nv):
                        s.pool(lambda e, w=w, v=v, j=j, wd=wd: e.tensor_tensor(wd[v][:, j, :], w[:], gf[v][:],
                                                                               ALU.mult), [w, gf[v]], [wd[v]])
                for ci, (t0, n) in enumerate(tchunks):
                    ab = ci % 2
                    hkk = [("hfT", t) for t in range(t0 // 128, (t0 + n) // 128)]
                    for j in range(g):
                        pg = rot.next()
                        pu = rot.next()
                        for (pp, wsrc) in ((pg, wg[pb]), (pu, wu[pb])):
                            for k in range(8):
                                s.pe(lambda e, pp=pp, wsrc=wsrc, k=k, j=j, t0=t0, n=n: e.matmul(
                                    pp[:, 0:n], wsrc[:, k, j * 128:(j + 1) * 128], hfT[:, k, t0:t0 + n],
                                    start=(k == 0), stop=(k == 7)), [wsrc] + hkk, [pp])
                        sb_ = sg[j % 2]
                        s.act(lambda e, pg=pg, sb_=sb_, n=n: e.activation(sb_[:, 0:n], pg[:, 0:n], AF.Silu),
                              [pg], [sb_])
                        s.dve(lambda e, pu=pu, sb_=sb_, n=n, j=j, ab=ab: e.tensor_tensor(
                            actT[ab][:, j, 0:n], sb_[:, 0:n], pu[:, 0:n], ALU.mult), [sb_, pu], [actT[ab]])
                    for tt in range(n // 128):
                        ti = t0 // 128 + tt
                        v = 0 if ti < NT else 1
                        for half in range(2):
                            pd = rot.next()
                            cs = slice(half * 512, (half + 1) * 512)
                            for j in range(g):
                                s.pe(lambda e, pd=pd, j=j, tt=tt, ab=ab, v=v, cs=cs, g=g, wd=wd: e.matmul(
                                    pd[:], actT[ab][:, j, tt * 128:(tt + 1) * 128], wd[v][:, j, cs],
                                    start=(j == 0), stop=(j == g - 1)), [actT[ab], wd[v]], [pd])
                            s.dve(lambda e, pd=pd, ti=ti, cs=cs: e.tensor_tensor(
                                xres[:, ti, cs], xres[:, ti, cs], pd[:], ALU.add), [pd, ("x", ti)], [("x", ti)])
            s.flush()
        with contextlib.ExitStack() as s3:
            if last:
                fw = c.sb(s3, "fw", [128, D])
                junk = [c.sb(s3, f"junk{i}", [128, D]) for i in range(2)]
                ss = c.sb(s3, "sso", [128, NTC])
                rs = c.sb(s3, "rso", [128, NTC])
                s.dma("sp", fw[:], rowsB[7].partition_broadcast(128), writes=[fw])
                for ti in range(NTT):
                    b = ti % 2
                    s.act(lambda e, b=b, ti=ti: e.activation(junk[b][:], xres[:, ti, :], AF.Square,
                                                             accum_out=ss[:, ti:ti + 1]),
                          [("x", ti)], [junk[b], ("sso", ti)])
                    s.act(lambda e, ti=ti: e.activation(rs[:, ti:ti + 1], ss[:, ti:ti + 1], AF.Sqrt, bias=EPS,
                                                        scale=1.0 / D), [("sso", ti)], [("rso", ti)])
                    s.dve(lambda e, ti=ti: e.reciprocal(rs[:, ti:ti + 1], rs[:, ti:ti + 1]),
                          [("rso", ti)], [("rso", ti)])
                    s.dve(lambda e, b=b, ti=ti: e.scalar_tensor_tensor(
                        junk[b][:], xres[:, ti, :], rs[:, ti:ti + 1], fw[:], ALU.mult, ALU.mult),
                        [("x", ti), ("rso", ti), fw, junk[b]], [junk[b]])
                    s.dma("sp", xo[ti * 128:(ti + 1) * 128, :], junk[b][:], reads=[junk[b]], writes=["xo"])
            else:
                for ti in range(NTT):
                    s.dma("sp", xo[ti * 128:(ti + 1) * 128, :], xres[:, ti, :], reads=[("x", ti)], writes=["xo"])
            s.flush()


def run_B(inp, layer, mod, xfull, cx, resA, ctx_out, last):
    _, _, g_m, sh_f, sc_f, g_f = mod_parts(mod, layer, 0)
    _, _, cg_m, csh_f, csc_f, cg_f = mod_parts(mod, layer, 1)
    rowsB = np.ascontiguousarray(np.stack([g_m, cg_m, g_f, cg_f, sc_f, csc_f, inp["norm_ffn_w"][layer],
                                           inp["final_norm_w"]]).astype(np.float32))
    colvB = np.ascontiguousarray(np.stack([col_layout(sh_f), col_layout(csh_f)], axis=-1).reshape(128, 16))
    S_all = np.ascontiguousarray(np.stack([r["o_S"] for r in resA]))
    D_all = np.ascontiguousarray(np.stack([r["o_D"] for r in resA]))
    kt_all = np.ascontiguousarray(np.concatenate([r["o_gqT"][:, 2, 0:TOK] for r in resA]
                                                 + [resA[0]["o_gqT"][:, 2, TOK:TOK + CTX]], axis=1))
    v_all = np.ascontiguousarray(np.concatenate([r["o_gv"][:, 0:NT, :] for r in resA]
                                                + [resA[0]["o_gv"][:, NT:NTC, :]], axis=1))
    tri = tri_consts()
    maps = []
    for k in range(NCORE):
        r = resA[k]
        sel = np.zeros((128, 8), np.float32)
        sel[:, k] = 1.0
        xown = np.ascontiguousarray(np.concatenate([xfull[k * TOK:(k + 1) * TOK], cx], axis=0))
        maps.append(dict(
            xown=xown, yacc_d=r["o_yacc"], zs_d=r["o_zs"], yaT_d=r["o_yaT"], CT_d=r["o_CT"], e2_d=r["o_e2"],
            S_all=S_all, D_all=D_all, hc_d=r["o_hc"], sel_d=sel,
            gqT_d=np.ascontiguousarray(r["o_gqT"][:, 0:2, :]), kt_all=kt_all, v_all=v_all,
            wout=inp["w_out"][layer], wgate=inp["ffn_w_gate"][layer], wup=inp["ffn_w_up"][layer],
            wdown=inp["ffn_w_down"][layer], rowsB=rowsB, ssdnw=inp["ssd_norm_w"][layer], colvB=colvB, tri=tri))
    res = run_bass_kernel_spmd(build_B(ctx_out, last), maps, core_ids=list(range(NCORE)))
    return res.results


def kernel(**inputs):
    inp = {k: np.asarray(v) for k, v in inputs.items()}
    mod = run_M(inp)
    x = np.ascontiguousarray(inp["x"][0])
    cx = np.ascontiguousarray(inp["ctx"][0])
    for layer in range(2):
        ctx_out = layer == 0
        resA = run_A(inp, layer, mod, x, cx, ctx_out)
        resB = run_B(inp, layer, mod, x, cx, resA, ctx_out, last=(layer == 1))
        x = np.ascontiguousarray(np.concatenate([r["xo"][:TOK] for r in resB], axis=0))
        if ctx_out:
            cx = np.ascontiguousarray(resB[0]["xo"][TOK:TOK + CTX])
    return x[None].astype(np.float32)
```
